# Optimizing a Trainium2 kernel written in Bass

```python
import jax, jax.numpy as jnp
from jax import lax
import numpy as np

D_MODEL = 1024
BATCH = 8
SEQ = 4096
DEPTH = 1

SGU_CHUNK = 128
SGU_GROUPS = 4
SGU_GROUP_DIM = 128
SGU_WIDTH = SGU_GROUPS * SGU_GROUP_DIM
N_HEADS = 8
HEAD_DIM = 64
ATTN_WIDTH = N_HEADS * HEAD_DIM
IDX_HEADS = 8
IDX_DIM = 64
TOPK_MAX = 256
Q_BLOCK = 128
ROPE_THETA = 10000.0
D_FF = 4 * D_MODEL
N_BRANCHES = 2
EPS = 1e-6

IN_SIZES = (SGU_WIDTH, SGU_WIDTH, ATTN_WIDTH, ATTN_WIDTH, ATTN_WIDTH,
            IDX_HEADS * IDX_DIM, IDX_DIM, IDX_HEADS, N_BRANCHES * D_MODEL)
D_IN = 2 * SGU_WIDTH + 3 * ATTN_WIDTH + IDX_HEADS * IDX_DIM + IDX_DIM + IDX_HEADS + N_BRANCHES * D_MODEL

kernel_name = "hybrid_gated_gmlp_dsa_block"


def rmsnorm(x, g):
    xf = x.astype(jnp.float32)
    y = xf * lax.rsqrt(jnp.mean(xf * xf, axis=-1, keepdims=True) + EPS)
    return (y * g.astype(jnp.float32)).astype(x.dtype)


def layernorm(x, g, b):
    xf = x.astype(jnp.float32)
    mu = jnp.mean(xf, axis=-1, keepdims=True)
    xc = xf - mu
    y = xc * lax.rsqrt(jnp.mean(xc * xc, axis=-1, keepdims=True) + EPS)
    return (y * g.astype(jnp.float32) + b.astype(jnp.float32)).astype(x.dtype)


def rope_tables(seq_len, dim, dtype):
    pos = jnp.arange(seq_len, dtype=jnp.float32)
    inv_freq = ROPE_THETA ** (-jnp.arange(0, dim, 2, dtype=jnp.float32) / dim)
    ang = pos[:, None] * inv_freq[None, :]
    return jnp.cos(ang).astype(dtype)[:, None, :], jnp.sin(ang).astype(dtype)[:, None, :]


def apply_rope(x, cos, sin):
    x1, x2 = jnp.split(x, 2, axis=-1)
    return jnp.concatenate([x1 * cos - x2 * sin, x2 * cos + x1 * sin], axis=-1)


def split_cols(t, sizes):
    out, start = [], 0
    for s in sizes:
        out.append(t[..., start:start + s])
        start += s
    return out


def sgu_mixer(u, v, w_s, b_s, ln_g, ln_b):
    B, S, _ = u.shape
    n_chunks = S // SGU_CHUNK
    v = layernorm(v, ln_g, ln_b)
    vc = v.reshape(B, n_chunks, SGU_CHUNK, SGU_GROUPS, SGU_GROUP_DIM)
    causal = jnp.tril(jnp.ones((SGU_CHUNK, SGU_CHUNK), dtype=bool))
    ws = jnp.where(causal[None], w_s, jnp.zeros((), w_s.dtype))
    s = jnp.einsum('gts,bcsgd->bctgd', ws, vc) + b_s.T[:, :, None]
    return u * s.reshape(B, S, SGU_WIDTH)


def dsa_attention(q, k, v, q_idx, k_idx, w_idx):
    B, S = q.shape[0], q.shape[1]
    topk = min(TOPK_MAX, S // 4)
    n_blocks = S // Q_BLOCK
    key_pos = jnp.arange(S)
    idx_scale = (IDX_DIM ** -0.5) * (IDX_HEADS ** -0.5)
    attn_scale = HEAD_DIM ** -0.5
    k_idx32 = k_idx.astype(jnp.float32)

    def block(i):
        start = i * Q_BLOCK
        qb = lax.dynamic_slice_in_dim(q, start, Q_BLOCK, axis=1)
        qib = lax.dynamic_slice_in_dim(q_idx, start, Q_BLOCK, axis=1)
        wib = lax.dynamic_slice_in_dim(w_idx, start, Q_BLOCK, axis=1)
        q_pos = start + jnp.arange(Q_BLOCK)
        logits = jnp.einsum('bthd,bsd->bths', qib.astype(jnp.float32), k_idx32)
        score = jnp.einsum('bth,bths->bts', wib.astype(jnp.float32) * idx_scale, jax.nn.relu(logits))
        causal = key_pos[None, :] <= q_pos[:, None]
        score = jnp.where(causal[None], score, -jnp.inf)
        _, sel = lax.top_k(score, topk)
        k_sel = jax.vmap(lambda kk, ii: kk[ii])(k, sel)
        v_sel = jax.vmap(lambda vv, ii: vv[ii])(v, sel)
        valid = sel <= q_pos[None, :, None]
        att = jnp.einsum('bthd,btkhd->bthk', qb, k_sel).astype(jnp.float32) * attn_scale
        att = jnp.where(valid[:, :, None, :], att, -jnp.inf)
        p = jax.nn.softmax(att, axis=-1).astype(v.dtype)
        return jnp.einsum('bthk,btkhd->bthd', p, v_sel)

    out = lax.map(block, jnp.arange(n_blocks))
    return out.transpose(1, 0, 2, 3, 4).reshape(B, S, ATTN_WIDTH)


def setup_inputs(seed: int = 0) -> dict:
    key = jax.random.key(seed)
    ks = jax.random.split(key, 16)
    f32 = jnp.float32
    n = lambda k, shape, scale: jax.random.normal(k, shape, f32) * scale
    return {
        "x": n(ks[0], (BATCH, SEQ, D_MODEL), 1.0),
        "norm1_g": 1.0 + n(ks[1], (DEPTH, D_MODEL), 0.02),
        "w_in": n(ks[2], (DEPTH, D_MODEL, D_IN), D_MODEL ** -0.5),
        "w_s": n(ks[3], (DEPTH, SGU_GROUPS, SGU_CHUNK, SGU_CHUNK), SGU_CHUNK ** -0.5),
        "b_s": 1.0 + n(ks[4], (DEPTH, SGU_GROUPS, SGU_CHUNK), 0.02),
        "sgu_ln_g": 1.0 + n(ks[5], (DEPTH, SGU_WIDTH), 0.02),
        "sgu_ln_b": n(ks[6], (DEPTH, SGU_WIDTH), 0.02),
        "w_out_a": n(ks[7], (DEPTH, SGU_WIDTH, D_MODEL), SGU_WIDTH ** -0.5),
        "w_out_b": n(ks[8], (DEPTH, ATTN_WIDTH, D_MODEL), ATTN_WIDTH ** -0.5),
        "w_o": n(ks[9], (DEPTH, D_MODEL, D_MODEL), D_MODEL ** -0.5),
        "norm2_g": 1.0 + n(ks[10], (DEPTH, D_MODEL), 0.02),
        "w_ff_in": n(ks[11], (DEPTH, D_MODEL, D_FF), D_MODEL ** -0.5),
        "w_ff_out": n(ks[12], (DEPTH, D_FF, D_MODEL), D_FF ** -0.5),
        "norm_f_g": 1.0 + n(ks[13], (D_MODEL,), 0.02),
    }


def reference(x, norm1_g, w_in, w_s, b_s, sgu_ln_g, sgu_ln_b, w_out_a, w_out_b, w_o,
              norm2_g, w_ff_in, w_ff_out, norm_f_g):
    B, S, _ = x.shape
    cos, sin = rope_tables(S, HEAD_DIM, x.dtype)
    for layer in range(DEPTH):
        h = rmsnorm(x, norm1_g[layer])
        proj = h @ w_in[layer]
        u, v_g, q, k, v, q_idx, k_idx, w_idx, gates = split_cols(proj, IN_SIZES)
        u = jax.nn.gelu(u)
        v_g = jax.nn.gelu(v_g)
        y_a = sgu_mixer(u, v_g, w_s[layer], b_s[layer], sgu_ln_g[layer], sgu_ln_b[layer])
        q = apply_rope(q.reshape(B, S, N_HEADS, HEAD_DIM), cos, sin)
        k = apply_rope(k.reshape(B, S, N_HEADS, HEAD_DIM), cos, sin)
        v = v.reshape(B, S, N_HEADS, HEAD_DIM)
        q_idx = apply_rope(q_idx.reshape(B, S, IDX_HEADS, IDX_DIM), cos, sin)
        k_idx = apply_rope(k_idx[:, :, None, :], cos, sin)[:, :, 0, :]
        y_b = dsa_attention(q, k, v, q_idx, k_idx, w_idx)
        g = jax.nn.sigmoid(gates.reshape(B, S, N_BRANCHES, D_MODEL))
        merged = g[:, :, 0, :] * (y_a @ w_out_a[layer]) + g[:, :, 1, :] * (y_b @ w_out_b[layer])
        x = x + merged @ w_o[layer]
        h2 = rmsnorm(x, norm2_g[layer])
        x = x + jnp.square(jax.nn.relu(h2 @ w_ff_in[layer])) @ w_ff_out[layer]
    return rmsnorm(x, norm_f_g)
```

```python
import numpy as np
import ml_dtypes
from contextlib import ExitStack
import concourse.bass as bass
import concourse.mybir as mybir
from concourse.bass_utils import run_bass_kernel_spmd

F32 = mybir.dt.float32
BF16 = mybir.dt.bfloat16
AF = mybir.ActivationFunctionType
ALU = mybir.AluOpType
AX = mybir.AxisListType

D = 1024
D_IN = 5192
EPS = 1e-6
IDX_SCALE = (64 ** -0.5) * (8 ** -0.5)
TOPK = 256
NEG = -1.0e30


class _Probe:
    def __init__(self):
        self.calls = []

    def __getattr__(self, name):
        def f(*a, **k):
            self.calls.append((name, a, k))
            return self
        return f


def _free(ap):
    try:
        sh = ap.shape
        n = 1
        for v in sh[1:]:
            n *= int(v)
        return n
    except Exception:
        return 128


class Prog:
    ENGS = ("pe", "act", "dve", "pool", "sp")

    def __init__(self, nc, es):
        self.nc = nc
        self.es = es
        self.all = []
        self.buf = {}
        self.dsem = {}
        self.esem = {}
        self.reorder = True

    def _deps(self, r, w):
        deps = set()
        for k in r:
            st = self.buf.setdefault(k, [None, []])
            if st[0] is not None:
                deps.add(st[0])
        for k in w:
            st = self.buf.setdefault(k, [None, []])
            if st[0] is not None:
                deps.add(st[0])
            deps.update(st[1])
        return deps

    def _commit(self, oid, r, w):
        for k in r:
            if k in w:
                continue
            self.buf[k][1].append(oid)
        for k in w:
            self.buf[k] = [oid, []]

    ALIAS = {"T2": ("T2a", "T2b"), ("Mb", 0): (("MbD", 0), ("MbA", 0)), ("Mb", 1): (("MbD", 1), ("MbA", 1)), ("Mb", 2): (("MbD", 2), ("MbA", 2))}

    def _excl(self, r, w):
        r = [kk for k in r for kk in self.ALIAS.get(k, (k,))]
        w = [kk for k in w for kk in self.ALIAS.get(k, (k,))]
        for k in list(r):
            if isinstance(k, tuple) and k and k[0] == "ps":
                r.remove(k)
                if k not in w:
                    w.append(k)
        return tuple(r), tuple(w)

    def _cost(self, eng, fn):
        p = _Probe()
        try:
            fn(p)
        except Exception:
            return 200.0
        if not p.calls:
            return 100.0
        name, a, k = p.calls[0]
        out = a[0] if a else k.get("out")
        n = _free(out)
        if eng == "pe":
            return 30.0 + 0.45 * max(n, 64)
        if eng == "act":
            return 170.0 + 0.75 * n
        if eng == "dve":
            c = 70.0 + 1.05 * n
            if "accum_out" in k and k["accum_out"] is not None:
                c += 70.0
            return c
        if eng == "pool":
            return 150.0 + 2.0 * n
        return 100.0

    def op(self, eng, fn, r=(), w=(), cost=None):
        r, w = self._excl(r, w)
        deps = self._deps(r, w)
        oid = len(self.all)
        self.all.append({"id": oid, "eng": eng, "fn": fn, "deps": deps, "dma": None,
                         "cost": self._cost(eng, fn) if cost is None else cost, "xfer": 0.0})
        self._commit(oid, r, w)
        return oid

    def dma(self, eng, fns, sem, r=(), w=(), nbytes=None):
        r, w = self._excl(r, w)
        if sem not in self.dsem:
            self.dsem[sem] = self.es.enter_context(self.nc.semaphore("d_" + sem))
        deps = self._deps(r, w)
        oid = len(self.all)
        if nbytes is None:
            nbytes = 0
            for f in fns:
                p = _Probe()
                try:
                    f(p)
                    name, a, k = p.calls[0]
                    out = k.get("out", a[0] if a else None)
                    nbytes += _free(out) * 128 * 4
                except Exception:
                    nbytes += 1 << 16
        self.all.append({"id": oid, "eng": eng, "fn": None, "deps": deps, "dma": (fns, sem),
                         "cost": (1000.0 if eng == "pool" else 80.0) * len(fns), "xfer": 2500.0 + nbytes / 160.0})
        self._commit(oid, r, w)
        return oid

    def schedule(self):
        import heapq
        ops = self.all
        n = len(ops)
        succ = [[] for _ in range(n)]
        npred = [0] * n
        for o in ops:
            for d in o["deps"]:
                succ[d].append(o["id"])
            npred[o["id"]] = len(o["deps"])
        done_t = [0.0] * n
        free_t = [0.0] * n
        rdy_t = [0.0] * n
        eng_free = {e: 0.0 for e in self.ENGS}
        ready = {e: [] for e in self.ENGS}
        for o in ops:
            if npred[o["id"]] == 0:
                heapq.heappush(ready[o["eng"]], o["id"])
        order = []
        W = 24 if self.reorder else 1
        cnt = 0
        while cnt < n:
            best = None
            for e in self.ENGS:
                h = ready[e]
                if not h:
                    continue
                cands = heapq.nsmallest(W, h)
                ef = eng_free[e]
                for oid in cands:
                    st = rdy_t[oid] if rdy_t[oid] > ef else ef
                    key = (st, oid)
                    if best is None or key < best[0]:
                        best = (key, e, oid)
            (st, oid), e, _ = best
            ready[e].remove(oid)
            heapq.heapify(ready[e])
            o = ops[oid]
            free_t[oid] = st + o["cost"]
            done_t[oid] = free_t[oid] + o["xfer"]
            eng_free[e] = free_t[oid]
            o["start"] = st
            order.append(oid)
            cnt += 1
            for s_ in succ[oid]:
                so = ops[s_]
                t = free_t[oid] if (so["eng"] == e == "pe" and o["dma"] is None) else done_t[oid]
                if t > rdy_t[s_]:
                    rdy_t[s_] = t
                npred[s_] -= 1
                if npred[s_] == 0:
                    heapq.heappush(ready[so["eng"]], s_)
        self.order = order
        self.est_total = max(done_t) if n else 0.0
        return order

    def emit(self, final_dmas):
        nc = self.nc
        ops = self.all
        order = self.schedule()
        for e in self.ENGS:
            self.esem[e] = self.es.enter_context(nc.semaphore("e_" + e))
        per = {e: [] for e in self.ENGS}
        pos = {}
        dcnt = {}
        for oid in order:
            o = ops[oid]
            pos[oid] = len(per[o["eng"]])
            per[o["eng"]].append(oid)
            if o["dma"] is not None:
                fns, sem = o["dma"]
                dcnt[sem] = dcnt.get(sem, 0) + 16 * len(fns)
                o["dval"] = dcnt[sem]
            o["sig"] = False
        seen = {e: {} for e in self.ENGS}
        for oid in order:
            o = ops[oid]
            e = o["eng"]
            sn = seen[e]
            emax = {}
            dmax = {}
            for d in o["deps"]:
                do = ops[d]
                if do["dma"] is not None:
                    sem = do["dma"][1]
                    if dmax.get(sem, -1) < do["dval"]:
                        dmax[sem] = do["dval"]
                else:
                    e2 = do["eng"]
                    if e2 == e and e == "pe":
                        continue
                    if e2 not in emax or pos[emax[e2]] < pos[d]:
                        emax[e2] = d
            waits = []
            for e2, d in emax.items():
                if sn.get(("e", e2), -1) >= pos[d]:
                    continue
                ops[d]["sig"] = True
                waits.append(("e", e2, d))
                for k2, v2 in ops[d]["clock"].items():
                    if sn.get(k2, -1) < v2:
                        sn[k2] = v2
                if sn.get(("e", e2), -1) < pos[d]:
                    sn[("e", e2)] = pos[d]
            for sem, val in dmax.items():
                if sn.get(("d", sem), -1) >= val:
                    continue
                waits.append(("d", sem, val))
                sn[("d", sem)] = val
            o["waits"] = waits
            o["clock"] = dict(sn)
        for e in self.ENGS:
            c = 0
            for oid in per[e]:
                if ops[oid]["sig"]:
                    c += 1
                ops[oid]["semval"] = c
        finals = [(ops[d]["dma"][1], ops[d]["dval"]) for d in final_dmas]
        self.per = per

        def run(engh, e):
            for oid in per[e]:
                o = ops[oid]
                for d in o["waits"]:
                    if d[0] == "e":
                        engh.wait_ge(self.esem[d[1]], ops[d[2]]["semval"])
                    else:
                        engh.wait_ge(self.dsem[d[1]], d[2])
                if o["dma"] is not None:
                    fns, sem = o["dma"]
                    for f in fns:
                        f(engh).then_inc(self.dsem[sem], 16)
                else:
                    inst = o["fn"](engh)
                    if o["sig"]:
                        inst.then_inc(self.esem[e], 1)
            if e == "sp":
                for (sem, val) in finals:
                    engh.wait_ge(self.dsem[sem], val)

        with nc.Block() as block:
            @block.tensor
            def _(t):
                run(t, "pe")

            @block.scalar
            def _(t):
                run(t, "act")

            @block.vector
            def _(t):
                run(t, "dve")

            @block.gpsimd
            def _(t):
                run(t, "pool")

            @block.sync
            def _(t):
                run(t, "sp")


def build(NT, G=2, NB=22, stop=None, dbg=(), dumpg=None):
    S = NT * 128
    NG = NT // G
    DG_ = (NG - 1) if dumpg is None else dumpg
    GT = G * 128
    nc = bass.Bass("TRN2", target_bir_lowering=False)
    es = ExitStack()
    P = Prog(nc, es)

    def din(name, shape, dt=F32):
        return nc.dram_tensor(name, list(shape), dt, kind="ExternalInput").ap()

    x = din("x", [S, D])
    g1 = din("g1", [1, D])
    w_in = din("w_in", [D, D_IN])
    w_s = din("w_s", [4, 128, 128])
    b_s = din("b_s", [4, 128])
    ln_g = din("ln_g", [1, 512])
    ln_b = din("ln_b", [1, 512])
    w_oa = din("w_oa", [512, D])
    w_ob = din("w_ob", [512, D])
    w_o = din("w_o", [D, D])
    g2 = din("g2", [1, D])
    w_f1 = din("w_f1", [D, 4096])
    w_f2 = din("w_f2", [4096, D])
    gf = din("gf", [1, D])
    c_ident = din("c_ident", [128, 128], BF16)
    c_caus = din("c_caus", [128, 128])
    c_tril = din("c_tril", [128, 128])
    c_rope = din("c_rope", [S, 96])
    y = nc.dram_tensor("y", [S, D], F32, kind="ExternalOutput").ap()

    def sb(name, shape, dt):
        return es.enter_context(nc.sbuf_tensor(name, list(shape), dt))

    ident = sb("ident", [128, 128], BF16)
    causb = sb("causb", [128, 128], F32)
    g1buf = sb("g1buf", [128, D], F32)
    g2buf = sb("g2buf", [128, D], F32)
    lngb = sb("lngb", [128, 512], F32)
    lnbb = sb("lnbb", [128, 512], F32)
    wsT = sb("wsT", [128, 4, 128], BF16)
    bsT = sb("bsT", [128, 4], F32)
    nhalf = sb("nhalf", [128, 1], F32)
    KT = sb("KT", [128, 4, S], BF16)
    Vc = sb("Vc", [128, NT, 8, 65], BF16)
    KIT2 = sb("KIT2", [128, S], BF16)
    xs = sb("xs", [128, G, D], F32)
    cs = sb("cs", [128, G, 96], F32)
    hT2 = [sb(f"hT{k}", [128, 8, GT], BF16) for k in range(2)]
    gu = sb("gu", [128, G, 512], F32)
    gv = sb("gv", [128, G, 512], F32)
    QT = sb("QT", [128, G, 4, 128], BF16)
    QIT = sb("QIT", [128, G, 4, 128], BF16)
    th = sb("th", [128, G, 2048], BF16)
    wabs = sb("wabs", [128, G, 8], F32)
    sgnw = sb("sgnw", [128, G, 8], F32)
    score0 = sb("score0", [128, 4096], F32)
    scores = [score0, score0]
    NSLOT = 3
    wsl = [sb(f"wsl{s}", [128, 8, 512], BF16) for s in range(NSLOT)]
    hbf = sb("hbf", [128, D], BF16)
    vln = hbf[:, 0:512]
    Mbs = [sb(f"Mb{k}", [128, 4096], BF16) for k in range(2)]
    T01 = sb("T01", [128, 1024], F32)
    T2_ = sb("T2", [128, 512], F32)
    T = [T01[:, 0:512], T01[:, 512:1024], T2_[:]]
    HD = sb("HD", [128, 4, GT], BF16)
    TB = [sb(f"TB{k}", [128, 512], BF16) for k in range(2)]
    ya = TB[0]
    yb = TB[1]
    yaT = sb("yaT", [128, 4, 128], BF16)
    ybT = sb("ybT", [128, 4, 128], BF16)
    mbf = hbf
    mT = sb("mT", [128, 8, 128], BF16)
    R = [sb(f"R{k}", [128, 512], BF16) for k in range(2)]
    Dg = sb("Dg", [128, 8, 128], BF16)
    E = [sb(f"E{k}", [128, 1024], BF16) for k in range(2)]
    maskT = [sb(f"maskT{k}", [128, 128], BF16) for k in range(2)]
    st = sb("st", [128, 160], F32)
    w0cs = sb("w0cs", [128, 2, 32], F32)
    ctab = sb("ctab", [128, 32], F32)
    C_SS, C_RS, C_LO, C_W0, C_MID, C_CNT, C_T2, C_RMAX, C_RMIN, C_MEAN, C_VAR = 0, 2, 4, 5, 6, 7, 8, 9, 10, 11, 12
    C_BN = 16
    C_CMAX = 24
    C_RDEN = 40
    C_CONST = 50
    C_SA, C_TMP = 52, 53
    DVE_FRAC = 0.45

    pb = [es.enter_context(nc.psum_tensor(f"pb{k}", [128, 512], F32)) for k in range(8)]
    pbb = [p[:].bitcast(BF16) for p in pb]
    pb8 = [p[:].bitcast(mybir.dt.float8e4) for p in pb]

    def PS(k):
        return ("ps", k)

    dsem_ctr = [0]

    def dma_sp(fns, sem, r, w):
        return P.dma("sp", fns, sem, r, w)

    def bcast_rows(ap_row, n):
        return ap_row.partition_broadcast(128) if hasattr(ap_row, "partition_broadcast") else ap_row

    tr_ctr = [0]

    def transposes(src_aps, src_keys, dst_fn, dst_keys, evac_eng="act"):
        bank = 0
        tr_ctr[0] += 1
        n = len(src_aps)
        for q, a in enumerate(src_aps):
            P.op("pe", lambda e, a=a, q=q, bank=bank: e.transpose(pbb[bank][:, q * 128:(q + 1) * 128], a, ident[:]),
                 r=list(src_keys) + ["ident"], w=[PS(bank)])
        P.op(evac_eng, lambda e, bank=bank, n=n: dst_fn(e, pbb[bank][:, 0:n * 128]), r=[PS(bank)], w=dst_keys)

    def copy_on(e, out, in_):
        if hasattr(e, "activation") and not hasattr(e, "tensor_copy"):
            return e.activation(out, in_, AF.Copy)
        return e.tensor_copy(out, in_)

    def act_copy(e, out, in_):
        return e.activation(out, in_, AF.Copy)

    def rsqrt_ops(col_in, col_out, scale, eps, keys_in, keys_out):
        P.op("dve", lambda e: e.tensor_scalar(st[:, col_out:col_out + 1], st[:, col_in:col_in + 1], scale, eps,
                                              ALU.mult, ALU.add), r=keys_in, w=keys_out)
        P.op("pool", lambda e: e.tensor_tensor(st[:, col_out:col_out + 1], st[:, col_out:col_out + 1], nhalf[:],
                                               ALU.pow), r=list(keys_out) + ["nhalf"], w=keys_out)

    dma_sp([lambda e: e.dma_start(out=ident[:], in_=c_ident)], "c0", [], ["ident"])
    dma_sp([lambda e: e.dma_start(out=causb[:], in_=c_caus)], "c1", [], ["causb"])
    dma_sp([lambda e: e.dma_start(out=g1buf[:], in_=g1.partition_broadcast(128))], "c2", [], ["g1buf"])
    dma_sp([lambda e: e.dma_start(out=g2buf[:], in_=g2.partition_broadcast(128))], "c3", [], ["g2buf"])
    dma_sp([lambda e: e.dma_start(out=lngb[:], in_=ln_g.partition_broadcast(128))], "c5", [], ["lngb"])
    dma_sp([lambda e: e.dma_start(out=lnbb[:], in_=ln_b.partition_broadcast(128))], "c6", [], ["lnbb"])
    dma_sp([lambda e: e.dma_start(out=bsT[:], in_=b_s.rearrange("g t -> t g"), allow_slow_non_contiguous=True)],
           "c7", [], ["bsT"])
    wsf = T[0][:].rearrange("p (g s) -> p g s", g=4)
    dma_sp([lambda e: e.dma_start(out=wsf, in_=w_s.rearrange("g t s -> t g s"))], "c8", [], ["T0"])
    dma_sp([lambda e: e.dma_start(out=T[1][:, 0:128], in_=c_tril)], "c9", [], ["T1"])
    P.op("dve", lambda e: e.tensor_tensor(TB[0][:].rearrange("p (g s) -> p g s", g=4), wsf,
                                          T[1][:, 0:128].unsqueeze(1).to_broadcast([128, 4, 128]), ALU.mult),
         r=["T0", "T1"], w=["TB0"])
    transposes([TB[0][:, g * 128:(g + 1) * 128] for g in range(4)], ["TB0"],
               lambda e, ps: act_copy(e, wsT[:].rearrange("p g t -> p (g t)"), ps), ["wsT"])
    for k_ in range(32):
        P.op("pool", lambda e, k_=k_: e.memset(ctab[:, k_:k_ + 1], 2.0 ** (-(k_ + 1))), w=["ctab"])
    P.op("pool", lambda e: e.memset(nhalf[:], -0.5), w=["nhalf"])
    P.op("pool", lambda e: e.memset(Vc[:].rearrange("p a b c -> p (a b c)"), 1.0), w=[("Vc", ii) for ii in range(NT)])
    P.op("pool", lambda e: e.memset(st[:, C_CONST:C_CONST + 1], -1.0e29), w=["st_const"])

    def dscr(name, shape):
        return nc.dram_tensor(name, list(shape), BF16, kind="Internal").ap()

    wt_in = dscr("wt_in", [11, 128, 8 * 512])
    wt_oa = dscr("wt_oa", [128, 4 * 1024])
    wt_ob = dscr("wt_ob", [128, 4 * 1024])
    wt_o = dscr("wt_o", [2, 128, 8 * 512])
    wt_f1 = dscr("wt_f1", [8, 128, 8 * 512])
    wt_f2 = dscr("wt_f2", [8, 128, 4 * 1024])
    PROJ_BLOCKS = [(6, 3072, 72), (5, 2560, 512), (0, 0, 512), (1, 512, 512), (2, 1024, 512), (3, 1536, 512),
                   (4, 2048, 512), (7, 3144, 512), (8, 3656, 512), (9, 4168, 512), (10, 4680, 512)]
    cv_ctr = [0]

    def conv(dst, src, key):
        cv_ctr[0] += 1
        P.dma("pool", [lambda e: e.dma_start(out=dst, in_=src)], f"cv{cv_ctr[0]}", [], [key])

    def in_view(b, ncol):
        return wt_in[b][:, 0:8 * ncol].rearrange("p (kc n) -> p kc n", kc=8)

    for (b, c0, ncol) in PROJ_BLOCKS:
        conv(in_view(b, ncol), w_in[:, c0:c0 + ncol].rearrange("(kc p) n -> p kc n", p=128), ("wb_in", b))
    CONV_LATER = []
    CONV_LATER.append((wt_oa.rearrange("p (kc n) -> p kc n", kc=4), w_oa.rearrange("(kc p) n -> p kc n", p=128), ("wb_oa",)))
    CONV_LATER.append((wt_ob.rearrange("p (kc n) -> p kc n", kc=4), w_ob.rearrange("(kc p) n -> p kc n", p=128), ("wb_ob",)))
    for hh in range(2):
        CONV_LATER.append((wt_o[hh].rearrange("p (kc n) -> p kc n", kc=8),
                           w_o[:, hh * 512:(hh + 1) * 512].rearrange("(kc p) n -> p kc n", p=128), ("wb_o", hh)))
    for fb in range(8):
        CONV_LATER.append((wt_f1[fb].rearrange("p (kc n) -> p kc n", kc=8),
                           w_f1[:, fb * 512:(fb + 1) * 512].rearrange("(kc p) n -> p kc n", p=128), ("wb_f1", fb)))
    for fb in range(8):
        CONV_LATER.append((wt_f2[fb].rearrange("p (fc n) -> p fc n", fc=4),
                           w_f2[fb * 512:(fb + 1) * 512, :].rearrange("(fc p) n -> p fc n", p=128), ("wb_f2", fb)))

    dbg_out = {}
    dbg_ids = []

    def dump(name, ap, keys, shape, dt=F32):
        dd = nc.dram_tensor("dbg_" + name, list(shape), dt, kind="ExternalOutput").ap()
        dbg_out[name] = dd
        t_ = dma_sp([lambda e: e.dma_start(out=dd, in_=ap)], "dbg_" + name, keys, [])
        dbg_ids.append(t_)
        return t_

    final_waits = []
    slot_ctr = [0]
    reserved = set()
    stream_slot = [0]

    def load_w(src_ap, shape3, ckey, slot=None):
        if slot is None:
            s = slot_ctr[0] % NSLOT
            slot_ctr[0] += 1
            if len(reserved) >= NSLOT:
                s = stream_slot[0]
            else:
                while s in reserved:
                    s = slot_ctr[0] % NSLOT
                    slot_ctr[0] += 1
        else:
            s = slot
        a, b, c = shape3
        dst = wsl[s][:].rearrange("p a b -> p (a b)")[:, 0:b * c].rearrange("p (b c) -> p b c", b=b)
        P.dma("sp", [lambda e: e.dma_start(out=dst, in_=src_ap)], f"wsl{s}", [ckey], [("wsl", s)], nbytes=128 * b * c * 2)
        return dst, ("wsl", s)

    pj_ctr = [0]

    def rope(src, H, j, out_ap, out_key, src_key, scale_ap=None, scale_key=None):
        s4 = src.rearrange("p (h two f) -> p h two f", h=H, two=2)
        cosb = cs[:, j, 0:32].unsqueeze(1).unsqueeze(1).to_broadcast([128, H, 2, 32])
        sinb = cs[:, j, 32:64].unsqueeze(1).to_broadcast([128, H, 32])
        nsinb = cs[:, j, 64:96].unsqueeze(1).to_broadcast([128, H, 32])
        t1 = T[0][:, 0:H * 64].rearrange("p (h two f) -> p h two f", h=H, two=2)
        t2 = T[1][:, 0:H * 64].rearrange("p (h two f) -> p h two f", h=H, two=2)
        ck = ("cs", j)
        P.op("dve", lambda e: e.tensor_tensor(t1, s4, cosb, ALU.mult), r=[src_key, ck], w=["T0"])
        P.op("dve", lambda e: e.tensor_tensor(t2[:, :, 0, :], s4[:, :, 1, :], nsinb, ALU.mult), r=[src_key, ck], w=["T1"])
        P.op("dve", lambda e: e.tensor_tensor(t2[:, :, 1, :], s4[:, :, 0, :], sinb, ALU.mult), r=[src_key, ck], w=["T1"])
        if scale_ap is None:
            P.op("dve", lambda e: e.tensor_tensor(out_ap, T[0][:, 0:H * 64], T[1][:, 0:H * 64], ALU.add),
                 r=["T0", "T1"], w=[out_key])
        else:
            P.op("dve", lambda e: e.tensor_tensor(T[0][:, 0:H * 64], T[0][:, 0:H * 64], T[1][:, 0:H * 64], ALU.add),
                 r=["T0", "T1"], w=["T0"])
            P.op("dve", lambda e: e.tensor_tensor(out_ap.rearrange("p (h f) -> p h f", h=H),
                                                  T[0][:, 0:H * 64].rearrange("p (h f) -> p h f", h=H),
                                                  scale_ap.unsqueeze(2).to_broadcast([128, H, 64]), ALU.mult),
                 r=["T0", scale_key], w=[out_key])

    gstate = [None]

    def rmsnorm_to_T(j, gbuf_, gkey, src_ap, src_keys, hTb, dstT_cols, dst_key):
        P.op("act", lambda e: e.activation(hbf[:], src_ap, AF.Square, accum_out=st[:, C_SS + j:C_SS + j + 1]),
             r=list(src_keys), w=["hbf", ("ss", j)])
        rsqrt_ops(C_SS + j, C_RS + j, 1.0 / D, EPS, [("ss", j)], [("rs", j)])
        P.op("dve", lambda e: e.scalar_tensor_tensor(hbf[:], src_ap, st[:, C_RS + j:C_RS + j + 1], gbuf_[:],
                                                     ALU.mult, ALU.mult), r=list(src_keys) + [("rs", j), gkey], w=["hbf"])
        transposes([hbf[:, kc * 128:(kc + 1) * 128] for kc in range(8)], ["hbf"],
                   lambda e, ps: act_copy(e, hTb[:, :, dstT_cols], ps.rearrange("p (k t) -> p k t", k=8)),
                   [dst_key])

    def proj_block(g, b, c0, ncol):
        wv, wk = load_w(in_view(b, ncol), (128, 8, ncol), ("wb_in", b))
        for j in range(G):
            i = g * G + j
            bank = 1 + (pj_ctr[0] % 2)
            pj_ctr[0] += 1
            pj = pb[bank]
            pk = PS(bank)
            for kc in range(8):
                P.op("pe", lambda e, pj=pj, kc=kc, j=j, wv=wv, ncol=ncol: e.matmul(
                    pj[:, 0:ncol], hT2[g % 2][:, kc, j * 128:(j + 1) * 128], wv[:, kc, :], start=(kc == 0), stop=(kc == 7)),
                    r=[("hT", g % 2, j), wk], w=[pk])
            if b == 6:
                rope(pj[:, 0:64], 1, j, TB[0][:, 0:64], "TB0", pk)
                P.op("dve", lambda e: e.tensor_copy(TB[0][:, 64:128], TB[0][:, 0:64]), r=["TB0"], w=["TB0"])
                P.op("act", lambda e, pj=pj, j=j: e.activation(wabs[:, j, :], pj[:, 64:72], AF.Abs, scale=IDX_SCALE),
                     r=[pk], w=[("wabs", j)])
                P.op("act", lambda e, pj=pj, j=j: e.activation(sgnw[:, j, :], pj[:, 64:72], AF.Sign),
                     r=[pk], w=[("sgnw", j)])
                transposes([TB[0][:, 0:128]], ["TB0"],
                           lambda e, ps, i=i: act_copy(e, KIT2[:, i * 128:(i + 1) * 128], ps), [("KIT", i)])
            elif b in (0, 1):
                dst = gu if b == 0 else gv
                dk = ("gu", j) if b == 0 else ("gv", j)
                P.op("act", lambda e, pj=pj: e.activation(T[2][:], pj[:], AF.Square), r=[pk], w=["T2"])
                P.op("dve", lambda e: e.tensor_scalar(T[2][:], T[2][:], 0.044715, 1.0, ALU.mult, ALU.add),
                     r=["T2"], w=["T2"])
                P.op("dve", lambda e, pj=pj: e.tensor_tensor(T[2][:], T[2][:], pj[:], ALU.mult), r=["T2", pk], w=["T2"])
                P.op("act", lambda e: e.activation(T[2][:], T[2][:], AF.Tanh, scale=0.7978845608028654),
                     r=["T2"], w=["T2"])
                P.op("dve", lambda e, pj=pj, dst=dst, j=j: e.scalar_tensor_tensor(dst[:, j, :], T[2][:], 1.0, pj[:],
                                                                                  ALU.add, ALU.mult),
                     r=["T2", pk], w=[dk])
            elif b == 2:
                rope(pj[:], 8, j, TB[0][:], "TB0", pk)
                transposes([TB[0][:, q * 128:(q + 1) * 128] for q in range(4)], ["TB0"],
                           lambda e, ps, j=j: act_copy(e, QT[:, j, :, :], ps.rearrange("p (q t) -> p q t", q=4)),
                           [("QT", j)])
            elif b == 3:
                rope(pj[:], 8, j, TB[1][:], "TB1", pk)
                transposes([TB[1][:, q * 128:(q + 1) * 128] for q in range(4)], ["TB1"],
                           lambda e, ps, i=i: act_copy(e, KT[:, :, i * 128:(i + 1) * 128],
                                                       ps.rearrange("p (q t) -> p q t", q=4)), [("KT", i)])
            elif b == 4:
                P.op("act", lambda e, pj=pj, i=i: act_copy(e, Vc[:, i, :, 0:64], pj[:].rearrange("p (h f) -> p h f", h=8)),
                     r=[pk], w=[("Vc", i)])
            elif b == 5:
                rope(pj[:], 8, j, TB[0][:], "TB0", pk, scale_ap=wabs[:, j, :], scale_key=("wabs", j))
                transposes([TB[0][:, q * 128:(q + 1) * 128] for q in range(4)], ["TB0"],
                           lambda e, ps, j=j: act_copy(e, QIT[:, j, :, :], ps.rearrange("p (q t) -> p q t", q=4)),
                           [("QIT", j)])
            else:
                o0 = (b - 7) * 512
                P.op("act", lambda e, pj=pj, j=j, o0=o0: e.activation(th[:, j, o0:o0 + 512], pj[:], AF.Tanh, scale=0.5),
                     r=[pk], w=[("th", j, b)])


    def tile_part(g, j, part):
        i = g * G + j
        n = (i + 1) * 128
        m3 = i % 2
        Mb = Mbs[m3]
        mbk = ("Mb", m3)
        score = scores[j]
        W0C = w0cs[:, j, :]
        o_ = 64 + j * 40
        C_LO, C_W0, C_MID, C_CNT, C_T2, C_RMAX, C_SA, C_TMP, C_CMAX = (o_, o_ + 1, o_ + 2, o_ + 3, o_ + 4, o_ + 5, o_ + 6,
                                                                      o_ + 7, o_ + 8)
        if part == "sgu":
            P.op("dve", lambda e, j=j: e.bn_stats(st[:, C_BN:C_BN + 6], gv[:, j, :]), r=[("gv", j)], w=["bn"])
            P.op("dve", lambda e: e.bn_aggr(st[:, C_MEAN:C_MEAN + 2], st[:, C_BN:C_BN + 6]), r=["bn"], w=["mv"])
            rsqrt_ops(C_VAR, C_VAR, 1.0, 4.0 * EPS, ["mv"], ["mv"])
            P.op("dve", lambda e, j=j: e.tensor_scalar(T[2][:], gv[:, j, :], st[:, C_MEAN:C_MEAN + 1],
                                                       st[:, C_VAR:C_VAR + 1], ALU.subtract, ALU.mult),
                 r=[("gv", j), "mv"], w=["T2"])
            P.op("dve", lambda e: e.tensor_tensor(T[2][:], T[2][:], lngb[:], ALU.mult), r=["T2", "lngb"], w=["T2"])
            P.op("dve", lambda e: e.tensor_tensor(vln[:], T[2][:], lnbb[:], ALU.add), r=["T2", "lnbb"], w=["hbf"])
            for q in range(4):
                P.op("pe", lambda e, q=q: e.matmul(pb[1][:, q * 128:(q + 1) * 128], wsT[:, q, :],
                                                   vln[:, q * 128:(q + 1) * 128], start=True, stop=True),
                     r=["wsT", "hbf"], w=[PS(1)])
            for q in range(4):
                P.op("dve", lambda e, q=q, j=j: e.scalar_tensor_tensor(
                    ya[:, q * 128:(q + 1) * 128], pb[1][:, q * 128:(q + 1) * 128], bsT[:, q:q + 1],
                    gu[:, j, q * 128:(q + 1) * 128], ALU.add, ALU.mult), r=[PS(1), "bsT", ("gu", j)], w=["TB0"])
            transposes([ya[:, q * 128:(q + 1) * 128] for q in range(4)], ["TB0"],
                       lambda e, ps: act_copy(e, yaT[:].rearrange("p q t -> p (q t)"), ps), ["yaT"])
            if stop == "sgu":
                if g == DG_:
                    dump(f"ya{j}", ya[:], ["TB0"], [128, 512], BF16)
                return
        if part == "index":
            P.op("dve", lambda e, j=j: e.tensor_tensor(Dg[:], ident[:].unsqueeze(1).to_broadcast([128, 8, 128]),
                                                       sgnw[:, j, :].unsqueeze(2).to_broadcast([128, 8, 128]), ALU.mult),
                 r=["ident", ("sgnw", j)], w=["Dg"])
            nchunk = (n + 511) // 512
            P.op("dve", lambda e: e.memset(st[:, C_CMAX:C_CMAX + 9], -3.0e38), w=[("cmax", j)])
            for c in range(nchunk):
                wdt = min(512, n - c * 512)
                sbank = 7
                for h in range(8):
                    lbank = 1 + (h % 2)
                    ee = h % 2
                    pp = h // 2
                    P.op("pe", lambda e, lbank=lbank, ee=ee, pp=pp, j=j, c=c, wdt=wdt: e.matmul(
                        pb[lbank][:, 0:wdt], QIT[ee * 64:(ee + 1) * 64, j, pp, :],
                        KIT2[ee * 64:(ee + 1) * 64, c * 512:c * 512 + wdt], start=True, stop=True),
                        r=[("QIT", j)] + [("KIT", kk) for kk in range(c * 4, min(c * 4 + 4, i + 1))], w=[PS(lbank)])
                    rb = h % 2
                    P.op("act", lambda e, rb=rb, lbank=lbank, wdt=wdt: e.activation(R[rb][:, 0:wdt], pb[lbank][:, 0:wdt], AF.Relu),
                         r=[PS(lbank)], w=[("R", rb)])
                    P.op("pe", lambda e, sbank=sbank, h=h, rb=rb, wdt=wdt: e.matmul(
                        pb[sbank][:, 0:wdt], Dg[:, h, :], R[rb][:, 0:wdt], start=(h == 0), stop=(h == 7)),
                        r=["Dg", ("R", rb)], w=[PS(sbank)])
                is_last = (c == nchunk - 1)
                wnd = wdt - 128 if is_last else wdt
                if wnd > 0:
                    P.op("dve", lambda e, sbank=sbank, c=c, wnd=wnd: e.tensor_scalar(
                        score[:, c * 512:c * 512 + wnd], pb[sbank][:, 0:wnd], 1.0, None, ALU.mult, ALU.max,
                        accum_out=st[:, C_CMAX + c:C_CMAX + c + 1]),
                        r=[PS(sbank)], w=[("sc", kk) for kk in range(c * 4, c * 4 + wnd // 128)] + [("cmax", j)])
                if is_last:
                    P.op("dve", lambda e, sbank=sbank, wnd=wnd, i=i: e.tensor_tensor(
                        score[:, i * 128:(i + 1) * 128], pb[sbank][:, wnd:wnd + 128], causb[:], ALU.add),
                        r=[PS(sbank), "causb"], w=[("sc", i)])
                    P.op("dve", lambda e, i=i: e.tensor_reduce(st[:, C_CMAX + 8:C_CMAX + 9], score[:, i * 128:(i + 1) * 128],
                                                                AX.X, ALU.max), r=[("sc", i)], w=[("cmax", j)])
            sck = [("sc", kk) for kk in range(i + 1)]
            if i >= 2:
                P.op("dve", lambda e: e.tensor_reduce(st[:, C_RMAX:C_RMAX + 1], st[:, C_CMAX:C_CMAX + 9], AX.X, ALU.max),
                     r=[("cmax", j)], w=[("rmax", j)])
                P.op("dve", lambda e, i=i, Mb=Mb: e.tensor_scalar(Mb[:, 0:i * 128], score[:, 0:i * 128], 1.0, None, ALU.mult,
                                                           ALU.min, accum_out=st[:, C_LO:C_LO + 1]),
                     r=sck, w=[mbk, ("lo", j)])
                P.op("dve", lambda e: e.tensor_tensor(st[:, C_W0:C_W0 + 1], st[:, C_RMAX:C_RMAX + 1], st[:, C_LO:C_LO + 1],
                                                      ALU.subtract), r=[("rmax", j), ("lo", j)], w=[("w0", j)])
                nd = max(128, int(round((0.444 * n - 430.0) / 128.0)) * 128)
                nd = min(nd, n - 128)
                na = n - nd
                P.op("dve", lambda e, W0C=W0C: e.tensor_scalar(W0C, ctab[:], st[:, C_W0:C_W0 + 1], None, ALU.mult),
                     r=[("w0", j), "ctab"], w=[("w0c", j)])
                P.op("dve", lambda e: e.tensor_scalar(st[:, C_MID:C_MID + 1], st[:, C_W0:C_W0 + 1], 0.5,
                                                      st[:, C_LO:C_LO + 1], ALU.mult, ALU.add),
                     r=[("w0", j), ("lo", j)], w=[("mid", j)])
                for k in range(NB):
                    P.op("dve", lambda e, nd=nd, Mb=Mb: e.tensor_scalar(Mb[:, 0:nd], score[:, 0:nd], st[:, C_MID:C_MID + 1], None,
                                                                 ALU.is_ge, ALU.add, accum_out=st[:, C_CNT:C_CNT + 1]),
                         r=sck + [("mid", j)], w=[("MbD", m3), ("cnt", j)])
                    P.op("act", lambda e, nd=nd, n=n, Mb=Mb: e.activation(Mb[:, nd:n], score[:, nd:n], AF.Sign,
                                                                   bias=st[:, C_MID:C_MID + 1], scale=-1.0,
                                                                   accum_out=st[:, C_SA:C_SA + 1]),
                         r=sck + [("mid", j)], w=[("MbA", m3), ("sa", j)])
                    P.op("dve", lambda e: e.scalar_tensor_tensor(st[:, C_TMP:C_TMP + 1], st[:, C_CNT:C_CNT + 1], 2.0,
                                                                 st[:, C_SA:C_SA + 1], ALU.mult, ALU.subtract),
                         r=[("cnt", j), ("sa", j)], w=[("tmp", j)])
                    P.op("dve", lambda e, na=na, k=k, W0C=W0C: e.tensor_scalar(st[:, C_T2:C_T2 + 1], st[:, C_TMP:C_TMP + 1],
                                                                 float(2 * TOPK - na), W0C[:, k:k + 1], ALU.is_ge, ALU.mult),
                         r=[("tmp", j), ("w0c", j)], w=[("t2", j)])
                    kk2 = k + 1 if k + 1 < NB else k
                    dstc = C_MID if k + 1 < NB else C_LO
                    P.op("dve", lambda e, kk2=kk2, dstc=dstc, W0C=W0C: e.tensor_scalar(
                        st[:, dstc:dstc + 1], st[:, C_MID:C_MID + 1], W0C[:, kk2:kk2 + 1], st[:, C_T2:C_T2 + 1],
                        ALU.subtract, ALU.add),
                        r=[("mid", j), ("w0c", j), ("t2", j)], w=[("mid", j) if k + 1 < NB else ("lo", j)])
                lo_ap = st[:, C_LO:C_LO + 1]
                lok = ("lo", j)
            else:
                lo_ap = st[:, C_CONST:C_CONST + 1]
                lok = "st_const"
            P.op("dve", lambda e, n=n, lo_ap=lo_ap: e.tensor_scalar(Mb[:, 0:n], score[:, 0:n], lo_ap, None, ALU.is_ge),
                 r=sck + [lok], w=[mbk])
            if stop == "index":
                if g == DG_:
                    dump(f"score{j}", score[:, 0:n], sck, [128, n])
                    dump(f"Mb{j}", Mb[:, 0:n], [mbk], [128, n], BF16)
                    dump(f"st{j}", st[:], [("lo", j), ("cnt", j), ("rmax", j), ("cmax", j)] if i >= 2 else [("cmax", j)], [128, 64])
                return
        if part == "attn":
            for jt in range(i + 1):
                ms = jt % 2
                transposes([Mb[:, jt * 128:(jt + 1) * 128]], [mbk],
                           lambda e, ps, ms=ms: act_copy(e, maskT[ms][:], ps), [("maskT", ms)])
                sb0 = 1 if (jt % 2 == 0) else 3
                for h in range(8):
                    ee = h % 2
                    pp = h // 2
                    bank = sb0 + ee
                    P.op("pe", lambda e, bank=bank, ee=ee, pp=pp, jt=jt, j=j: e.matmul(
                        pb[bank][:, pp * 128:(pp + 1) * 128], KT[ee * 64:(ee + 1) * 64, pp, jt * 128:(jt + 1) * 128],
                        QT[ee * 64:(ee + 1) * 64, j, pp, :], start=True, stop=True),
                        r=[("KT", jt), ("QT", j)], w=[PS(bank)])
                eb = jt % 2
                for hh in range(2):
                    P.op("act", lambda e, eb=eb, hh=hh, sb0=sb0: e.activation(E[eb][:, hh * 512:(hh + 1) * 512], pb[sb0 + hh][:],
                                                                             AF.Exp, scale=0.125),
                         r=[PS(sb0 + hh)], w=[("E", eb, hh)])
                P.op("dve", lambda e, eb=eb, ms=ms: e.tensor_tensor(
                    E[eb][:].rearrange("p (h t) -> p h t", h=8), E[eb][:].rearrange("p (h t) -> p h t", h=8),
                    maskT[ms][:].unsqueeze(1).to_broadcast([128, 8, 128]), ALU.mult),
                    r=[("E", eb, 0), ("E", eb, 1), ("maskT", ms)], w=[("E", eb, 0), ("E", eb, 1)])
                for blk in range(8):
                    ee = blk // 4
                    pp = blk % 4
                    h = 2 * pp + ee
                    bank = 5 + ee
                    c0 = pp * 65
                    P.op("pe", lambda e, bank=bank, c0=c0, h=h, blk=blk, eb=eb, jt=jt, pp=pp, i=i: e.matmul(
                        pb[bank][:, c0:c0 + 65], E[eb][:, blk * 128:(blk + 1) * 128], Vc[:, jt, h, :],
                        start=(jt == 0 and pp == 0), stop=(jt == i), skip_group_check=True),
                        r=[("E", eb, 0), ("E", eb, 1), ("Vc", jt)], w=[PS(bank)])
            for ee in range(2):
                ov = pb[5 + ee][:, 0:260].rearrange("p (h c) -> p h c", h=4)
                ybv = yb[:].rearrange("p (pp ee f) -> p ee pp f", pp=4, ee=2)[:, ee]
                P.op("dve", lambda e, ov=ov, ee=ee: e.tensor_copy(st[:, C_RDEN + ee * 4:C_RDEN + ee * 4 + 4], ov[:, :, 64]),
                     r=[PS(5 + ee)], w=["rden"])
                P.op("dve", lambda e, ee=ee: e.reciprocal(st[:, C_RDEN + ee * 4:C_RDEN + ee * 4 + 4],
                                                          st[:, C_RDEN + ee * 4:C_RDEN + ee * 4 + 4]), r=["rden"], w=["rden"])
                P.op("dve", lambda e, ov=ov, ee=ee, ybv=ybv: e.tensor_tensor(
                    ybv, ov[:, :, 0:64],
                    st[:, C_RDEN + ee * 4:C_RDEN + ee * 4 + 4].unsqueeze(2).to_broadcast([128, 4, 64]), ALU.mult),
                    r=[PS(5 + ee), "rden"], w=["TB1"])
            transposes([yb[:, q * 128:(q + 1) * 128] for q in range(4)], ["TB1"],
                       lambda e, ps: act_copy(e, ybT[:].rearrange("p q t -> p (q t)"), ps), ["ybT"])
            if stop == "attn":
                if g == DG_:
                    dump(f"yb{j}", yb[:], ["TB1"], [128, 512], BF16)
                return
        if part == "merge":
            for hh in range(2):
                for kc in range(4):
                    P.op("pe", lambda e, hh=hh, kc=kc, WA=WA: e.matmul(pb[1 + hh][:], yaT[:, kc, :], WA[:, kc, hh * 512:(hh + 1) * 512],
                                                                start=(kc == 0), stop=(kc == 3)),
                         r=["yaT", wak], w=[PS(1 + hh)])
                for kc in range(4):
                    P.op("pe", lambda e, hh=hh, kc=kc, WB=WB: e.matmul(pb[3 + hh][:], ybT[:, kc, :], WB[:, kc, hh * 512:(hh + 1) * 512],
                                                                start=(kc == 0), stop=(kc == 3)),
                         r=["ybT", wbk], w=[PS(3 + hh)])
                P.op("dve", lambda e, hh=hh, j=j: e.scalar_tensor_tensor(T[2][:], th[:, j, hh * 512:(hh + 1) * 512], 1.0,
                                                                         pb[1 + hh][:], ALU.add, ALU.mult),
                     r=[("th", j, 7 + hh), PS(1 + hh)], w=["T2"])
                P.op("dve", lambda e, hh=hh, j=j: e.scalar_tensor_tensor(T[1][:], th[:, j, 1024 + hh * 512:1024 + (hh + 1) * 512],
                                                                         1.0, pb[3 + hh][:], ALU.add, ALU.mult),
                     r=[("th", j, 9 + hh), PS(3 + hh)], w=["T1"])
                P.op("dve", lambda e, hh=hh: e.scalar_tensor_tensor(mbf[:, hh * 512:(hh + 1) * 512], T[2][:], 0.5, T[1][:],
                                                                    ALU.mult, ALU.add), r=["T2", "T1"], w=["hbf"])
            transposes([mbf[:, kc * 128:(kc + 1) * 128] for kc in range(8)], ["hbf"],
                       lambda e, ps: act_copy(e, mT[:].rearrange("p k t -> p (k t)"), ps), ["mT"])
            for hh in range(2):
                WO, wok = load_w(wt_o[hh].rearrange("p (kc n) -> p kc n", kc=8), (128, 8, 512),
                                 ("wb_o", hh), slot=sC)
                for kc in range(8):
                    P.op("pe", lambda e, hh=hh, kc=kc, WO=WO: e.matmul(pb[5 + hh][:], mT[:, kc, :], WO[:, kc, :],
                                                                       start=(kc == 0), stop=(kc == 7)),
                         r=["mT", wok], w=[PS(5 + hh)])
                P.op("dve", lambda e, hh=hh, j=j: e.scalar_tensor_tensor(xs[:, j, hh * 512:(hh + 1) * 512], pb[5 + hh][:], 0.5,
                                                                         xs[:, j, hh * 512:(hh + 1) * 512], ALU.mult, ALU.add),
                     r=[PS(5 + hh), ("xs", j)], w=[("xs", j)])
            if stop == "merge":
                if g == DG_:
                    dump(f"mbf{j}", mbf[:], ["hbf"], [128, D], BF16)
                    dump(f"thd{j}", th[:, j, :], [("th", j, b) for b in (7, 8, 9, 10)], [128, 2048], BF16)
                    dump(f"yaT{j}", yaT[:], ["yaT"], [128, 4, 128], BF16)
                    dump(f"ybT{j}", ybT[:], ["ybT"], [128, 4, 128], BF16)
                    dump(f"x1_{j}", xs[:, j, :], [("xs", j)], [128, D])
                return
            rmsnorm_to_T(j, g2buf, "g2buf", xs[:, j, :], [("xs", j)], hT2[g % 2], slice(j * 128, (j + 1) * 128), ("hT", g % 2, j))

    def ffn_and_store(gp):
        hTb = hT2[gp % 2]
        hk = [("hT", gp % 2, j) for j in range(G)]
        for fb in range(8):
            W1, w1k = load_w(wt_f1[fb].rearrange("p (kc n) -> p kc n", kc=8), (128, 8, 512),
                             ("wb_f1", fb))
            W2, w2k = load_w(wt_f2[fb].rearrange("p (fc n) -> p fc n", fc=4),
                             (128, 4, 1024), ("wb_f2", fb))
            for fcl in range(4):
                fc = fb * 4 + fcl
                hbank = 1 + (fc % 2)
                hb = fc % 2
                for kc in range(8):
                    P.op("pe", lambda e, hbank=hbank, kc=kc, fcl=fcl, W1=W1, hTb=hTb: e.matmul(
                        pb[hbank][:, 0:GT], W1[:, kc, fcl * 128:(fcl + 1) * 128], hTb[:, kc, :],
                        start=(kc == 0), stop=(kc == 7)), r=hk + [w1k], w=[PS(hbank)])
                tkey = "T2a" if hb == 0 else "T2b"
                P.op("act", lambda e, hbank=hbank, hb=hb: e.activation(T2_[:, hb * GT:(hb + 1) * GT], pb[hbank][:, 0:GT], AF.Relu),
                     r=[PS(hbank)], w=[tkey])
                hd = fc % 4
                P.op("pool", lambda e, hb=hb, hd=hd: e.tensor_tensor(HD[:, hd, :], T2_[:, hb * GT:(hb + 1) * GT],
                                                                     T2_[:, hb * GT:(hb + 1) * GT], ALU.mult),
                     r=[tkey], w=[("HD", hd)])
                for j in range(G):
                    for half in range(2):
                        obank = 3 + 2 * j + half
                        P.op("pe", lambda e, obank=obank, fc=fc, fcl=fcl, j=j, W2=W2, hd=hd, half=half: e.matmul(
                            pb[obank][:], HD[:, hd, j * 128:(j + 1) * 128], W2[:, fcl, half * 512:(half + 1) * 512],
                            start=(fc == 0), stop=(fc == 31)), r=[("HD", hd), w2k], w=[PS(obank)])
        for j in range(G):
            for half in range(2):
                obank = 3 + 2 * j + half
                P.op("dve", lambda e, obank=obank, j=j, half=half: e.tensor_tensor(
                    xs[:, j, half * 512:(half + 1) * 512], xs[:, j, half * 512:(half + 1) * 512], pb[obank][:], ALU.add),
                    r=[PS(obank), ("xs", j)], w=[("xs", j)])
        P.dma("pool", [lambda e: e.dma_start(out=T01[:], in_=gf.partition_broadcast(128))], "xtmp", [], ["T0", "T1"])
        for j in range(G):
            i = gp * G + j
            xk = ("xs", j)
            P.op("act", lambda e, j=j: e.activation(hbf[:], xs[:, j, :], AF.Square, accum_out=st[:, C_SS + j:C_SS + j + 1]),
                 r=[xk], w=["hbf", ("ss", j)])
            rsqrt_ops(C_SS + j, C_RS + j, 1.0 / D, EPS, [("ss", j)], [("rs", j)])
            P.op("dve", lambda e, j=j: e.scalar_tensor_tensor(xs[:, j, :], xs[:, j, :], st[:, C_RS + j:C_RS + j + 1], T01[:],
                                                              ALU.mult, ALU.mult), r=[xk, ("rs", j), "T0", "T1"], w=[xk])
            tok = P.dma("pool", [lambda e, i=i, j=j: e.dma_start(out=y[i * 128:(i + 1) * 128, :], in_=xs[:, j, :])],
                        f"out{j}", [xk], [])
            final_waits.append(tok)

    def early_part(g):
        for j in range(G):
            i = g * G + j
            P.dma("pool", [lambda e, i=i: e.dma_start(out=T01[:], in_=x[i * 128:(i + 1) * 128, :])],
                  "xtmp", [], ["T0", "T1"])
            P.dma("pool", [lambda e, i=i, j=j: e.dma_start(out=cs[:, j, :], in_=c_rope[i * 128:(i + 1) * 128, :])],
                  f"cs{j}", [], [("cs", j)])
            rmsnorm_to_T(j, g1buf, "g1buf", T01[:], ["T0", "T1"], hT2[g % 2], slice(j * 128, (j + 1) * 128), ("hT", g % 2, j))
        if g == 0:
            for (dst_, src_, key_) in CONV_LATER:
                conv(dst_, src_, key_)
        for blk_ in PROJ_BLOCKS[:2]:
            proj_block(g, *blk_)

    for g in range(NG):
        if g == 0:
            early_part(0)
            tile_part(0, 0, "index")
            tile_part(0, 1, "index")
        for j in range(G):
            i = g * G + j
            P.dma("pool", [lambda e, i=i, j=j: e.dma_start(out=xs[:, j, :], in_=x[i * 128:(i + 1) * 128, :])],
                  f"xs{j}", [], [("xs", j)])
        for blk_ in PROJ_BLOCKS[2:]:
            proj_block(g, *blk_)
        sA = slot_ctr[0] % NSLOT
        sB = (slot_ctr[0] + 1) % NSLOT
        sC = (slot_ctr[0] + 2) % NSLOT
        slot_ctr[0] += 3
        reserved.update((sA, sB, sC))
        stream_slot[0] = sC
        WA, wak = load_w(wt_oa.rearrange("p (kc n) -> p kc n", kc=4), (128, 4, 1024), ("wb_oa",), slot=sA)
        WB, wbk = load_w(wt_ob.rearrange("p (kc n) -> p kc n", kc=4), (128, 4, 1024), ("wb_ob",), slot=sB)
        tile_part(g, 0, "sgu")
        if g + 1 < NG:
            early_part(g + 1)
        tile_part(g, 0, "attn")
        tile_part(g, 0, "merge")
        tile_part(g, 1, "sgu")
        tile_part(g, 1, "attn")
        tile_part(g, 1, "merge")
        reserved.clear()
        if g + 1 < NG:
            tile_part(g + 1, 0, "index")
        ffn_and_store(g)
        if g + 1 < NG:
            tile_part(g + 1, 1, "index")

    final_waits.extend(dbg_ids)
    P.emit(final_waits)
    return nc, es, list(dbg_out.keys())


def host_consts(S):
    ident = np.eye(128, dtype=np.float32).astype(ml_dtypes.bfloat16)
    t = np.arange(128)[:, None]
    s = np.arange(128)[None, :]
    caus = np.where(s <= t, 0.0, NEG).astype(np.float32)
    tril = (s <= t).astype(np.float32)
    pos = np.arange(S, dtype=np.float32)
    inv_freq = (np.float32(10000.0) ** (-np.arange(0, 64, 2, dtype=np.float32) / np.float32(64))).astype(np.float32)
    ang = (pos[:, None] * inv_freq[None, :]).astype(np.float32)
    cos = np.cos(ang).astype(np.float32)
    sin = np.sin(ang).astype(np.float32)
    rope = np.concatenate([cos, sin, -sin], axis=1).astype(np.float32)
    return ident, caus, tril, rope


def make_in_maps(inputs, NT, ncores=8):
    S = NT * 128
    ident, caus, tril, rope = host_consts(S)
    f = lambda a: np.ascontiguousarray(np.asarray(a, dtype=np.float32))
    shared = {
        "g1": f(inputs["norm1_g"]).reshape(1, D),
        "w_in": f(inputs["w_in"])[0],
        "w_s": f(inputs["w_s"])[0],
        "b_s": f(inputs["b_s"])[0],
        "ln_g": f(inputs["sgu_ln_g"]).reshape(1, 512),
        "ln_b": f(inputs["sgu_ln_b"]).reshape(1, 512),
        "w_oa": f(inputs["w_out_a"])[0],
        "w_ob": f(inputs["w_out_b"])[0],
        "w_o": f(inputs["w_o"])[0],
        "g2": f(inputs["norm2_g"]).reshape(1, D),
        "w_f1": f(inputs["w_ff_in"])[0],
        "w_f2": f(inputs["w_ff_out"])[0],
        "gf": f(inputs["norm_f_g"]).reshape(1, D),
        "c_ident": ident, "c_caus": caus, "c_tril": tril, "c_rope": rope,
    }
    xx = np.asarray(inputs["x"], dtype=np.float32)
    maps = []
    for c in range(ncores):
        m = dict(shared)
        m["x"] = np.ascontiguousarray(xx[c, :S, :])
        maps.append(m)
    return maps


_CACHE = {}


def kernel(**inputs):
    NT = 32
    if "nc" not in _CACHE:
        _CACHE["nc"] = build(NT)
    nc, es, _ = _CACHE["nc"]
    in_maps = make_in_maps(inputs, NT)
    res = run_bass_kernel_spmd(nc, in_maps, core_ids=list(range(8)))
    out = np.stack([np.asarray(r["y"], dtype=np.float32) for r in res.results], axis=0)
    return out
```

```python
import numpy as np
import ml_dtypes
from contextlib import ExitStack
import concourse.bass as bass
import concourse.mybir as mybir
from concourse.bass_utils import run_bass_kernel_spmd

F32 = mybir.dt.float32
BF16 = mybir.dt.bfloat16
AF = mybir.ActivationFunctionType
ALU = mybir.AluOpType
AX = mybir.AxisListType

D = 1024
D_IN = 5192
EPS = 1e-6
IDX_SCALE = (64 ** -0.5) * (8 ** -0.5)
TOPK = 256
NEG = -1.0e30


class _Probe:
    def __init__(self):
        self.calls = []

    def __getattr__(self, name):
        def f(*a, **k):
            self.calls.append((name, a, k))
            return self
        return f


def _free(ap):
    try:
        sh = ap.shape
        n = 1
        for v in sh[1:]:
            n *= int(v)
        return n
    except Exception:
        return 128


class Prog:
    ENGS = ("pe", "act", "dve", "pool", "sp")

    def __init__(self, nc, es):
        self.nc = nc
        self.es = es
        self.all = []
        self.buf = {}
        self.dsem = {}
        self.esem = {}
        self.reorder = True
        self.relax_same = True

    def _deps(self, r, w):
        deps = set()
        raw = set()
        for k in r:
            st = self.buf.setdefault(k, [None, []])
            if st[0] is not None:
                deps.add(st[0])
                raw.add(st[0])
        for k in w:
            st = self.buf.setdefault(k, [None, []])
            if st[0] is not None:
                deps.add(st[0])
            deps.update(st[1])
        self._last_raw = raw
        return deps

    def _commit(self, oid, r, w):
        for k in r:
            if k in w:
                continue
            self.buf[k][1].append(oid)
        for k in w:
            self.buf[k] = [oid, []]

    ALIAS = {"T2": ("T2a", "T2b"), ("Mb", 0): (("MbD", 0), ("MbA", 0)), ("Mb", 1): (("MbD", 1), ("MbA", 1)), ("Mb", 2): (("MbD", 2), ("MbA", 2))}

    def _excl(self, r, w):
        r = [kk for k in r for kk in self.ALIAS.get(k, (k,))]
        w = [kk for k in w for kk in self.ALIAS.get(k, (k,))]
        for k in list(r):
            if isinstance(k, tuple) and k and k[0] == "ps":
                r.remove(k)
                if k not in w:
                    w.append(k)
        return tuple(r), tuple(w)

    def _cost(self, eng, fn):
        p = _Probe()
        try:
            fn(p)
        except Exception:
            return 200.0
        if not p.calls:
            return 100.0
        name, a, k = p.calls[0]
        out = a[0] if a else k.get("out")
        n = _free(out)
        if eng == "pe":
            return 30.0 + 0.45 * max(n, 64)
        if eng == "act":
            return 170.0 + 0.75 * n
        if eng == "dve":
            c = 70.0 + 1.05 * n
            if "accum_out" in k and k["accum_out"] is not None:
                c += 70.0
            return c
        if eng == "pool":
            return 150.0 + 2.0 * n
        return 100.0

    def op(self, eng, fn, r=(), w=(), cost=None):
        r, w = self._excl(r, w)
        deps = self._deps(r, w)
        oid = len(self.all)
        self.all.append({"id": oid, "eng": eng, "fn": fn, "deps": deps, "dma": None, "raw": self._last_raw,
                         "cost": self._cost(eng, fn) if cost is None else cost, "xfer": 0.0})
        self._commit(oid, r, w)
        return oid

    def dma(self, eng, fns, sem, r=(), w=(), nbytes=None):
        r, w = self._excl(r, w)
        if sem not in self.dsem:
            self.dsem[sem] = self.es.enter_context(self.nc.semaphore("d_" + sem))
        deps = self._deps(r, w)
        oid = len(self.all)
        if nbytes is None:
            nbytes = 0
            for f in fns:
                p = _Probe()
                try:
                    f(p)
                    name, a, k = p.calls[0]
                    out = k.get("out", a[0] if a else None)
                    nbytes += _free(out) * 128 * 4
                except Exception:
                    nbytes += 1 << 16
        self.all.append({"id": oid, "eng": eng, "fn": None, "deps": deps, "dma": (fns, sem), "raw": self._last_raw,
                         "cost": (1000.0 if eng == "pool" else 80.0) * len(fns), "xfer": 2500.0 + nbytes / 160.0})
        self._commit(oid, r, w)
        return oid

    def schedule(self):
        import heapq
        ops = self.all
        n = len(ops)
        succ = [[] for _ in range(n)]
        npred = [0] * n
        for o in ops:
            for d in o["deps"]:
                succ[d].append(o["id"])
            npred[o["id"]] = len(o["deps"])
        done_t = [0.0] * n
        free_t = [0.0] * n
        rdy_t = [0.0] * n
        eng_free = {e: 0.0 for e in self.ENGS}
        ready = {e: [] for e in self.ENGS}
        for o in ops:
            if npred[o["id"]] == 0:
                heapq.heappush(ready[o["eng"]], o["id"])
        order = []
        W = 24 if self.reorder else 1
        cnt = 0
        while cnt < n:
            best = None
            for e in self.ENGS:
                h = ready[e]
                if not h:
                    continue
                cands = heapq.nsmallest(W, h)
                ef = eng_free[e]
                for oid in cands:
                    st = rdy_t[oid] if rdy_t[oid] > ef else ef
                    key = (st, oid)
                    if best is None or key < best[0]:
                        best = (key, e, oid)
            (st, oid), e, _ = best
            ready[e].remove(oid)
            heapq.heapify(ready[e])
            o = ops[oid]
            free_t[oid] = st + o["cost"]
            done_t[oid] = free_t[oid] + o["xfer"]
            eng_free[e] = free_t[oid]
            o["start"] = st
            order.append(oid)
            cnt += 1
            for s_ in succ[oid]:
                so = ops[s_]
                t = free_t[oid] if (so["eng"] == e == "pe" and o["dma"] is None) else done_t[oid]
                if t > rdy_t[s_]:
                    rdy_t[s_] = t
                npred[s_] -= 1
                if npred[s_] == 0:
                    heapq.heappush(ready[so["eng"]], s_)
        self.order = order
        self.est_total = max(done_t) if n else 0.0
        return order

    def emit(self, final_dmas):
        nc = self.nc
        ops = self.all
        order = self.schedule()
        for e in self.ENGS:
            self.esem[e] = self.es.enter_context(nc.semaphore("e_" + e))
        per = {e: [] for e in self.ENGS}
        pos = {}
        dcnt = {}
        for oid in order:
            o = ops[oid]
            pos[oid] = len(per[o["eng"]])
            per[o["eng"]].append(oid)
            if o["dma"] is not None:
                fns, sem = o["dma"]
                dcnt[sem] = dcnt.get(sem, 0) + 16 * len(fns)
                o["dval"] = dcnt[sem]
            o["sig"] = False
        seen = {e: {} for e in self.ENGS}
        nskip = [0]
        self.nskip = nskip
        for oid in order:
            o = ops[oid]
            e = o["eng"]
            sn = seen[e]
            emax = {}
            dmax = {}
            for d in o["deps"]:
                do = ops[d]
                if do["dma"] is not None:
                    sem = do["dma"][1]
                    if dmax.get(sem, -1) < do["dval"]:
                        dmax[sem] = do["dval"]
                else:
                    e2 = do["eng"]
                    if e2 == e and e == "pe":
                        continue
                    if e2 == e and self.relax_same and e in ("act", "dve") and d not in o["raw"] and o["dma"] is None:
                        nskip[0] += 1
                        continue
                    if e2 not in emax or pos[emax[e2]] < pos[d]:
                        emax[e2] = d
            waits = []
            for e2, d in emax.items():
                if sn.get(("e", e2), -1) >= pos[d]:
                    continue
                ops[d]["sig"] = True
                waits.append(("e", e2, d))
                for k2, v2 in ops[d]["clock"].items():
                    if sn.get(k2, -1) < v2:
                        sn[k2] = v2
                if sn.get(("e", e2), -1) < pos[d]:
                    sn[("e", e2)] = pos[d]
            for sem, val in dmax.items():
                if sn.get(("d", sem), -1) >= val:
                    continue
                waits.append(("d", sem, val))
                sn[("d", sem)] = val
            o["waits"] = waits
            o["clock"] = dict(sn)
        for e in self.ENGS:
            c = 0
            for oid in per[e]:
                if ops[oid]["sig"]:
                    c += 1
                ops[oid]["semval"] = c
        finals = [(ops[d]["dma"][1], ops[d]["dval"]) for d in final_dmas]
        self.per = per

        def run(engh, e):
            for oid in per[e]:
                o = ops[oid]
                for d in o["waits"]:
                    if d[0] == "e":
                        engh.wait_ge(self.esem[d[1]], ops[d[2]]["semval"])
                    else:
                        engh.wait_ge(self.dsem[d[1]], d[2])
                if o["dma"] is not None:
                    fns, sem = o["dma"]
                    for f in fns:
                        f(engh).then_inc(self.dsem[sem], 16)
                else:
                    inst = o["fn"](engh)
                    if o["sig"]:
                        inst.then_inc(self.esem[e], 1)
            if e == "sp":
                for (sem, val) in finals:
                    engh.wait_ge(self.dsem[sem], val)

        with nc.Block() as block:
            @block.tensor
            def _(t):
                run(t, "pe")

            @block.scalar
            def _(t):
                run(t, "act")

            @block.vector
            def _(t):
                run(t, "dve")

            @block.gpsimd
            def _(t):
                run(t, "pool")

            @block.sync
            def _(t):
                run(t, "sp")


def build(NT, G=2, NB=22, stop=None, dbg=(), dumpg=None):
    S = NT * 128
    NG = NT // G
    DG_ = (NG - 1) if dumpg is None else dumpg
    GT = G * 128
    nc = bass.Bass("TRN2", target_bir_lowering=False)
    es = ExitStack()
    P = Prog(nc, es)

    def din(name, shape, dt=F32):
        return nc.dram_tensor(name, list(shape), dt, kind="ExternalInput").ap()

    x = din("x", [S, D])
    g1 = din("g1", [1, D])
    w_in = din("w_in", [D, D_IN])
    w_s = din("w_s", [4, 128, 128])
    b_s = din("b_s", [4, 128])
    ln_g = din("ln_g", [1, 512])
    ln_b = din("ln_b", [1, 512])
    w_oa = din("w_oa", [512, D])
    w_ob = din("w_ob", [512, D])
    w_o = din("w_o", [D, D])
    g2 = din("g2", [1, D])
    w_f1 = din("w_f1", [D, 4096])
    w_f2 = din("w_f2", [4096, D])
    gf = din("gf", [1, D])
    c_ident = din("c_ident", [128, 128], BF16)
    c_caus = din("c_caus", [128, 128])
    c_tril = din("c_tril", [128, 128])
    c_rope = din("c_rope", [S, 96])
    y = nc.dram_tensor("y", [S, D], F32, kind="ExternalOutput").ap()

    def sb(name, shape, dt):
        return es.enter_context(nc.sbuf_tensor(name, list(shape), dt))

    ident = sb("ident", [128, 128], BF16)
    causb = sb("causb", [128, 128], F32)
    g1buf = sb("g1buf", [128, D], F32)
    g2buf = sb("g2buf", [128, D], F32)
    lngb = sb("lngb", [128, 512], F32)
    lnbb = sb("lnbb", [128, 512], F32)
    wsT = sb("wsT", [128, 4, 128], BF16)
    bsT = sb("bsT", [128, 4], F32)
    nhalf = sb("nhalf", [128, 1], F32)
    KT = sb("KT", [128, 4, S], BF16)
    Vc = sb("Vc", [128, NT, 8, 65], BF16)
    KIT2 = sb("KIT2", [128, S], BF16)
    xs = sb("xs", [128, G, D], F32)
    cs = sb("cs", [128, G, 96], F32)
    hT2 = [sb(f"hT{k}", [128, 8, GT], BF16) for k in range(2)]
    gu = sb("gu", [128, G, 512], F32)
    gv = sb("gv", [128, G, 512], F32)
    QT = sb("QT", [128, G, 4, 128], BF16)
    QIT = sb("QIT", [128, G, 4, 128], BF16)
    th = sb("th", [128, G, 2048], BF16)
    wabs = sb("wabs", [128, G, 8], F32)
    sgnw = sb("sgnw", [128, G, 8], F32)
    score0 = sb("score0", [128, 4096], F32)
    scores = [score0, score0]
    NSLOT = 3
    wsl = [sb(f"wsl{s}", [128, 8, 512], BF16) for s in range(NSLOT)]
    hbf = sb("hbf", [128, D], BF16)
    vln = hbf[:, 0:512]
    Mbs = [sb(f"Mb{k}", [128, 4096], BF16) for k in range(2)]
    T01 = sb("T01", [128, 1024], F32)
    T2_ = sb("T2", [128, 512], F32)
    T = [T01[:, 0:512], T01[:, 512:1024], T2_[:]]
    HD = sb("HD", [128, 4, GT], BF16)
    TB = [sb(f"TB{k}", [128, 512], BF16) for k in range(2)]
    ya = TB[0]
    yb = TB[1]
    yaT = sb("yaT", [128, 4, 128], BF16)
    ybT = sb("ybT", [128, 4, 128], BF16)
    mbf = hbf
    mT = sb("mT", [128, 8, 128], BF16)
    R = [sb(f"R{k}", [128, 512], BF16) for k in range(2)]
    Dg = sb("Dg", [128, 8, 128], BF16)
    E = [sb(f"E{k}", [128, 1024], BF16) for k in range(2)]
    maskT = [sb(f"maskT{k}", [128, 128], BF16) for k in range(2)]
    st = sb("st", [128, 160], F32)
    w0cs = sb("w0cs", [128, 2, 32], F32)
    ctab = sb("ctab", [128, 32], F32)
    C_SS, C_RS, C_LO, C_W0, C_MID, C_CNT, C_T2, C_RMAX, C_RMIN, C_MEAN, C_VAR = 0, 2, 4, 5, 6, 7, 8, 9, 10, 11, 12
    C_BN = 16
    C_CMAX = 24
    C_RDEN = 40
    C_CONST = 50
    C_SA, C_TMP = 52, 53
    DVE_FRAC = 0.45

    pb = [es.enter_context(nc.psum_tensor(f"pb{k}", [128, 512], F32)) for k in range(8)]
    pbb = [p[:].bitcast(BF16) for p in pb]
    pb8 = [p[:].bitcast(mybir.dt.float8e4) for p in pb]

    def PS(k):
        return ("ps", k)

    dsem_ctr = [0]

    def dma_sp(fns, sem, r, w):
        return P.dma("sp", fns, sem, r, w)

    def bcast_rows(ap_row, n):
        return ap_row.partition_broadcast(128) if hasattr(ap_row, "partition_broadcast") else ap_row

    tr_ctr = [0]

    def transposes(src_aps, src_keys, dst_fn, dst_keys, evac_eng="act"):
        bank = 0
        tr_ctr[0] += 1
        n = len(src_aps)
        for q, a in enumerate(src_aps):
            P.op("pe", lambda e, a=a, q=q, bank=bank: e.transpose(pbb[bank][:, q * 128:(q + 1) * 128], a, ident[:]),
                 r=list(src_keys) + ["ident"], w=[PS(bank)])
        P.op(evac_eng, lambda e, bank=bank, n=n: dst_fn(e, pbb[bank][:, 0:n * 128]), r=[PS(bank)], w=dst_keys)

    def copy_on(e, out, in_):
        if hasattr(e, "activation") and not hasattr(e, "tensor_copy"):
            return e.activation(out, in_, AF.Copy)
        return e.tensor_copy(out, in_)

    def act_copy(e, out, in_):
        return e.activation(out, in_, AF.Copy)

    def rsqrt_ops(col_in, col_out, scale, eps, keys_in, keys_out):
        P.op("dve", lambda e: e.tensor_scalar(st[:, col_out:col_out + 1], st[:, col_in:col_in + 1], scale, eps,
                                              ALU.mult, ALU.add), r=keys_in, w=keys_out)
        P.op("pool", lambda e: e.tensor_tensor(st[:, col_out:col_out + 1], st[:, col_out:col_out + 1], nhalf[:],
                                               ALU.pow), r=list(keys_out) + ["nhalf"], w=keys_out)

    dma_sp([lambda e: e.dma_start(out=ident[:], in_=c_ident)], "c0", [], ["ident"])
    dma_sp([lambda e: e.dma_start(out=causb[:], in_=c_caus)], "c1", [], ["causb"])
    dma_sp([lambda e: e.dma_start(out=g1buf[:], in_=g1.partition_broadcast(128))], "c2", [], ["g1buf"])
    dma_sp([lambda e: e.dma_start(out=g2buf[:], in_=g2.partition_broadcast(128))], "c3", [], ["g2buf"])
    dma_sp([lambda e: e.dma_start(out=lngb[:], in_=ln_g.partition_broadcast(128))], "c5", [], ["lngb"])
    dma_sp([lambda e: e.dma_start(out=lnbb[:], in_=ln_b.partition_broadcast(128))], "c6", [], ["lnbb"])
    dma_sp([lambda e: e.dma_start(out=bsT[:], in_=b_s.rearrange("g t -> t g"), allow_slow_non_contiguous=True)],
           "c7", [], ["bsT"])
    wsf = T[0][:].rearrange("p (g s) -> p g s", g=4)
    dma_sp([lambda e: e.dma_start(out=wsf, in_=w_s.rearrange("g t s -> t g s"))], "c8", [], ["T0"])
    dma_sp([lambda e: e.dma_start(out=T[1][:, 0:128], in_=c_tril)], "c9", [], ["T1"])
    P.op("dve", lambda e: e.tensor_tensor(TB[0][:].rearrange("p (g s) -> p g s", g=4), wsf,
                                          T[1][:, 0:128].unsqueeze(1).to_broadcast([128, 4, 128]), ALU.mult),
         r=["T0", "T1"], w=["TB0"])
    transposes([TB[0][:, g * 128:(g + 1) * 128] for g in range(4)], ["TB0"],
               lambda e, ps: act_copy(e, wsT[:].rearrange("p g t -> p (g t)"), ps), ["wsT"])
    for k_ in range(32):
        P.op("pool", lambda e, k_=k_: e.memset(ctab[:, k_:k_ + 1], 2.0 ** (-(k_ + 1))), w=["ctab"])
    P.op("pool", lambda e: e.memset(nhalf[:], -0.5), w=["nhalf"])
    P.op("pool", lambda e: e.memset(Vc[:].rearrange("p a b c -> p (a b c)"), 1.0), w=[("Vc", ii) for ii in range(NT)])
    P.op("pool", lambda e: e.memset(st[:, C_CONST:C_CONST + 1], -1.0e29), w=["st_const"])

    def dscr(name, shape):
        return nc.dram_tensor(name, list(shape), BF16, kind="Internal").ap()

    wt_in = dscr("wt_in", [11, 128, 8 * 512])
    wt_oa = dscr("wt_oa", [128, 4 * 1024])
    wt_ob = dscr("wt_ob", [128, 4 * 1024])
    wt_o = dscr("wt_o", [2, 128, 8 * 512])
    wt_f1 = dscr("wt_f1", [8, 128, 8 * 512])
    wt_f2 = dscr("wt_f2", [8, 128, 4 * 1024])
    PROJ_BLOCKS = [(6, 3072, 72), (5, 2560, 512), (0, 0, 512), (1, 512, 512), (2, 1024, 512), (3, 1536, 512),
                   (4, 2048, 512), (7, 3144, 512), (8, 3656, 512), (9, 4168, 512), (10, 4680, 512)]
    cv_ctr = [0]

    def conv(dst, src, key):
        cv_ctr[0] += 1
        P.dma("pool", [lambda e: e.dma_start(out=dst, in_=src)], f"cv{cv_ctr[0]}", [], [key])

    def in_view(b, ncol):
        return wt_in[b][:, 0:8 * ncol].rearrange("p (kc n) -> p kc n", kc=8)

    for (b, c0, ncol) in PROJ_BLOCKS:
        conv(in_view(b, ncol), w_in[:, c0:c0 + ncol].rearrange("(kc p) n -> p kc n", p=128), ("wb_in", b))
    CONV_LATER = []
    CONV_LATER.append((wt_oa.rearrange("p (kc n) -> p kc n", kc=4), w_oa.rearrange("(kc p) n -> p kc n", p=128), ("wb_oa",)))
    CONV_LATER.append((wt_ob.rearrange("p (kc n) -> p kc n", kc=4), w_ob.rearrange("(kc p) n -> p kc n", p=128), ("wb_ob",)))
    for hh in range(2):
        CONV_LATER.append((wt_o[hh].rearrange("p (kc n) -> p kc n", kc=8),
                           w_o[:, hh * 512:(hh + 1) * 512].rearrange("(kc p) n -> p kc n", p=128), ("wb_o", hh)))
    for fb in range(8):
        CONV_LATER.append((wt_f1[fb].rearrange("p (kc n) -> p kc n", kc=8),
                           w_f1[:, fb * 512:(fb + 1) * 512].rearrange("(kc p) n -> p kc n", p=128), ("wb_f1", fb)))
    for fb in range(8):
        CONV_LATER.append((wt_f2[fb].rearrange("p (fc n) -> p fc n", fc=4),
                           w_f2[fb * 512:(fb + 1) * 512, :].rearrange("(fc p) n -> p fc n", p=128), ("wb_f2", fb)))

    dbg_out = {}
    dbg_ids = []

    def dump(name, ap, keys, shape, dt=F32):
        dd = nc.dram_tensor("dbg_" + name, list(shape), dt, kind="ExternalOutput").ap()
        dbg_out[name] = dd
        t_ = dma_sp([lambda e: e.dma_start(out=dd, in_=ap)], "dbg_" + name, keys, [])
        dbg_ids.append(t_)
        return t_

    final_waits = []
    slot_ctr = [0]
    reserved = set()
    stream_slot = [0]

    def load_w(src_ap, shape3, ckey, slot=None):
        if slot is None:
            s = slot_ctr[0] % NSLOT
            slot_ctr[0] += 1
            if len(reserved) >= NSLOT:
                s = stream_slot[0]
            else:
                while s in reserved:
                    s = slot_ctr[0] % NSLOT
                    slot_ctr[0] += 1
        else:
            s = slot
        a, b, c = shape3
        dst = wsl[s][:].rearrange("p a b -> p (a b)")[:, 0:b * c].rearrange("p (b c) -> p b c", b=b)
        P.dma("sp", [lambda e: e.dma_start(out=dst, in_=src_ap)], f"wsl{s}", [ckey], [("wsl", s)], nbytes=128 * b * c * 2)
        return dst, ("wsl", s)

    pj_ctr = [0]

    def rope(src, H, j, out_ap, out_key, src_key, scale_ap=None, scale_key=None):
        s4 = src.rearrange("p (h two f) -> p h two f", h=H, two=2)
        cosb = cs[:, j, 0:32].unsqueeze(1).unsqueeze(1).to_broadcast([128, H, 2, 32])
        sinb = cs[:, j, 32:64].unsqueeze(1).to_broadcast([128, H, 32])
        nsinb = cs[:, j, 64:96].unsqueeze(1).to_broadcast([128, H, 32])
        t1 = T[0][:, 0:H * 64].rearrange("p (h two f) -> p h two f", h=H, two=2)
        t2 = T[1][:, 0:H * 64].rearrange("p (h two f) -> p h two f", h=H, two=2)
        ck = ("cs", j)
        P.op("dve", lambda e: e.tensor_tensor(t1, s4, cosb, ALU.mult), r=[src_key, ck], w=["T0"])
        P.op("dve", lambda e: e.tensor_tensor(t2[:, :, 0, :], s4[:, :, 1, :], nsinb, ALU.mult), r=[src_key, ck], w=["T1"])
        P.op("dve", lambda e: e.tensor_tensor(t2[:, :, 1, :], s4[:, :, 0, :], sinb, ALU.mult), r=[src_key, ck], w=["T1"])
        if scale_ap is None:
            P.op("dve", lambda e: e.tensor_tensor(out_ap, T[0][:, 0:H * 64], T[1][:, 0:H * 64], ALU.add),
                 r=["T0", "T1"], w=[out_key])
        else:
            P.op("dve", lambda e: e.tensor_tensor(T[0][:, 0:H * 64], T[0][:, 0:H * 64], T[1][:, 0:H * 64], ALU.add),
                 r=["T0", "T1"], w=["T0"])
            P.op("dve", lambda e: e.tensor_tensor(out_ap.rearrange("p (h f) -> p h f", h=H),
                                                  T[0][:, 0:H * 64].rearrange("p (h f) -> p h f", h=H),
                                                  scale_ap.unsqueeze(2).to_broadcast([128, H, 64]), ALU.mult),
                 r=["T0", scale_key], w=[out_key])

    gstate = [None]

    def rmsnorm_to_T(j, gbuf_, gkey, src_ap, src_keys, hTb, dstT_cols, dst_key):
        P.op("act", lambda e: e.activation(hbf[:], src_ap, AF.Square, accum_out=st[:, C_SS + j:C_SS + j + 1]),
             r=list(src_keys), w=["hbf", ("ss", j)])
        rsqrt_ops(C_SS + j, C_RS + j, 1.0 / D, EPS, [("ss", j)], [("rs", j)])
        P.op("dve", lambda e: e.scalar_tensor_tensor(hbf[:], src_ap, st[:, C_RS + j:C_RS + j + 1], gbuf_[:],
                                                     ALU.mult, ALU.mult), r=list(src_keys) + [("rs", j), gkey], w=["hbf"])
        transposes([hbf[:, kc * 128:(kc + 1) * 128] for kc in range(8)], ["hbf"],
                   lambda e, ps: act_copy(e, hTb[:, :, dstT_cols], ps.rearrange("p (k t) -> p k t", k=8)),
                   [dst_key])

    def proj_block(g, b, c0, ncol):
        wv, wk = load_w(in_view(b, ncol), (128, 8, ncol), ("wb_in", b))
        for j in range(G):
            i = g * G + j
            bank = 1 + (pj_ctr[0] % 2)
            pj_ctr[0] += 1
            pj = pb[bank]
            pk = PS(bank)
            for kc in range(8):
                P.op("pe", lambda e, pj=pj, kc=kc, j=j, wv=wv, ncol=ncol: e.matmul(
                    pj[:, 0:ncol], hT2[g % 2][:, kc, j * 128:(j + 1) * 128], wv[:, kc, :], start=(kc == 0), stop=(kc == 7)),
                    r=[("hT", g % 2, j), wk], w=[pk])
            if b == 6:
                rope(pj[:, 0:64], 1, j, TB[0][:, 0:64], "TB0", pk)
                P.op("dve", lambda e: e.tensor_copy(TB[0][:, 64:128], TB[0][:, 0:64]), r=["TB0"], w=["TB0"])
                P.op("act", lambda e, pj=pj, j=j: e.activation(wabs[:, j, :], pj[:, 64:72], AF.Abs, scale=IDX_SCALE),
                     r=[pk], w=[("wabs", j)])
                P.op("act", lambda e, pj=pj, j=j: e.activation(sgnw[:, j, :], pj[:, 64:72], AF.Sign),
                     r=[pk], w=[("sgnw", j)])
                transposes([TB[0][:, 0:128]], ["TB0"],
                           lambda e, ps, i=i: act_copy(e, KIT2[:, i * 128:(i + 1) * 128], ps), [("KIT", i)])
            elif b in (0, 1):
                dst = gu if b == 0 else gv
                dk = ("gu", j) if b == 0 else ("gv", j)
                P.op("act", lambda e, pj=pj: e.activation(T[2][:], pj[:], AF.Square), r=[pk], w=["T2"])
                P.op("dve", lambda e: e.tensor_scalar(T[2][:], T[2][:], 0.044715, 1.0, ALU.mult, ALU.add),
                     r=["T2"], w=["T2"])
                P.op("dve", lambda e, pj=pj: e.tensor_tensor(T[2][:], T[2][:], pj[:], ALU.mult), r=["T2", pk], w=["T2"])
                P.op("act", lambda e: e.activation(T[2][:], T[2][:], AF.Tanh, scale=0.7978845608028654),
                     r=["T2"], w=["T2"])
                P.op("dve", lambda e, pj=pj, dst=dst, j=j: e.scalar_tensor_tensor(dst[:, j, :], T[2][:], 1.0, pj[:],
                                                                                  ALU.add, ALU.mult),
                     r=["T2", pk], w=[dk])
            elif b == 2:
                rope(pj[:], 8, j, TB[0][:], "TB0", pk)
                transposes([TB[0][:, q * 128:(q + 1) * 128] for q in range(4)], ["TB0"],
                           lambda e, ps, j=j: act_copy(e, QT[:, j, :, :], ps.rearrange("p (q t) -> p q t", q=4)),
                           [("QT", j)])
            elif b == 3:
                rope(pj[:], 8, j, TB[1][:], "TB1", pk)
                transposes([TB[1][:, q * 128:(q + 1) * 128] for q in range(4)], ["TB1"],
                           lambda e, ps, i=i: act_copy(e, KT[:, :, i * 128:(i + 1) * 128],
                                                       ps.rearrange("p (q t) -> p q t", q=4)), [("KT", i)])
            elif b == 4:
                P.op("act", lambda e, pj=pj, i=i: act_copy(e, Vc[:, i, :, 0:64], pj[:].rearrange("p (h f) -> p h f", h=8)),
                     r=[pk], w=[("Vc", i)])
            elif b == 5:
                rope(pj[:], 8, j, TB[0][:], "TB0", pk, scale_ap=wabs[:, j, :], scale_key=("wabs", j))
                transposes([TB[0][:, q * 128:(q + 1) * 128] for q in range(4)], ["TB0"],
                           lambda e, ps, j=j: act_copy(e, QIT[:, j, :, :], ps.rearrange("p (q t) -> p q t", q=4)),
                           [("QIT", j)])
            else:
                o0 = (b - 7) * 512
                P.op("act", lambda e, pj=pj, j=j, o0=o0: e.activation(th[:, j, o0:o0 + 512], pj[:], AF.Tanh, scale=0.5),
                     r=[pk], w=[("th", j, b)])


    def tile_part(g, j, part):
        i = g * G + j
        n = (i + 1) * 128
        m3 = i % 2
        Mb = Mbs[m3]
        mbk = ("Mb", m3)
        score = scores[j]
        W0C = w0cs[:, j, :]
        o_ = 64 + j * 40
        C_LO, C_W0, C_MID, C_CNT, C_T2, C_RMAX, C_SA, C_TMP, C_CMAX = (o_, o_ + 1, o_ + 2, o_ + 3, o_ + 4, o_ + 5, o_ + 6,
                                                                      o_ + 7, o_ + 8)
        if part == "sgu":
            P.op("dve", lambda e, j=j: e.bn_stats(st[:, C_BN:C_BN + 6], gv[:, j, :]), r=[("gv", j)], w=["bn"])
            P.op("dve", lambda e: e.bn_aggr(st[:, C_MEAN:C_MEAN + 2], st[:, C_BN:C_BN + 6]), r=["bn"], w=["mv"])
            rsqrt_ops(C_VAR, C_VAR, 1.0, 4.0 * EPS, ["mv"], ["mv"])
            P.op("dve", lambda e, j=j: e.tensor_scalar(T[2][:], gv[:, j, :], st[:, C_MEAN:C_MEAN + 1],
                                                       st[:, C_VAR:C_VAR + 1], ALU.subtract, ALU.mult),
                 r=[("gv", j), "mv"], w=["T2"])
            P.op("dve", lambda e: e.tensor_tensor(T[2][:], T[2][:], lngb[:], ALU.mult), r=["T2", "lngb"], w=["T2"])
            P.op("dve", lambda e: e.tensor_tensor(vln[:], T[2][:], lnbb[:], ALU.add), r=["T2", "lnbb"], w=["hbf"])
            for q in range(4):
                P.op("pe", lambda e, q=q: e.matmul(pb[1][:, q * 128:(q + 1) * 128], wsT[:, q, :],
                                                   vln[:, q * 128:(q + 1) * 128], start=True, stop=True),
                     r=["wsT", "hbf"], w=[PS(1)])
            for q in range(4):
                P.op("dve", lambda e, q=q, j=j: e.scalar_tensor_tensor(
                    ya[:, q * 128:(q + 1) * 128], pb[1][:, q * 128:(q + 1) * 128], bsT[:, q:q + 1],
                    gu[:, j, q * 128:(q + 1) * 128], ALU.add, ALU.mult), r=[PS(1), "bsT", ("gu", j)], w=["TB0"])
            transposes([ya[:, q * 128:(q + 1) * 128] for q in range(4)], ["TB0"],
                       lambda e, ps: act_copy(e, yaT[:].rearrange("p q t -> p (q t)"), ps), ["yaT"])
            if stop == "sgu":
                if g == DG_:
                    dump(f"ya{j}", ya[:], ["TB0"], [128, 512], BF16)
                return
        if part == "index":
            P.op("dve", lambda e, j=j: e.tensor_tensor(Dg[:], ident[:].unsqueeze(1).to_broadcast([128, 8, 128]),
                                                       sgnw[:, j, :].unsqueeze(2).to_broadcast([128, 8, 128]), ALU.mult),
                 r=["ident", ("sgnw", j)], w=["Dg"])
            nchunk = (n + 511) // 512
            P.op("dve", lambda e: e.memset(st[:, C_CMAX:C_CMAX + 9], -3.0e38), w=[("cmax", j)])
            for c in range(nchunk):
                wdt = min(512, n - c * 512)
                sbank = 7
                for h in range(8):
                    lbank = 1 + (h % 2)
                    ee = h % 2
                    pp = h // 2
                    P.op("pe", lambda e, lbank=lbank, ee=ee, pp=pp, j=j, c=c, wdt=wdt: e.matmul(
                        pb[lbank][:, 0:wdt], QIT[ee * 64:(ee + 1) * 64, j, pp, :],
                        KIT2[ee * 64:(ee + 1) * 64, c * 512:c * 512 + wdt], start=True, stop=True),
                        r=[("QIT", j)] + [("KIT", kk) for kk in range(c * 4, min(c * 4 + 4, i + 1))], w=[PS(lbank)])
                    rb = h % 2
                    P.op("act", lambda e, rb=rb, lbank=lbank, wdt=wdt: e.activation(R[rb][:, 0:wdt], pb[lbank][:, 0:wdt], AF.Relu),
                         r=[PS(lbank)], w=[("R", rb)])
                    P.op("pe", lambda e, sbank=sbank, h=h, rb=rb, wdt=wdt: e.matmul(
                        pb[sbank][:, 0:wdt], Dg[:, h, :], R[rb][:, 0:wdt], start=(h == 0), stop=(h == 7)),
                        r=["Dg", ("R", rb)], w=[PS(sbank)])
                is_last = (c == nchunk - 1)
                wnd = wdt - 128 if is_last else wdt
                if wnd > 0:
                    P.op("dve", lambda e, sbank=sbank, c=c, wnd=wnd: e.tensor_scalar(
                        score[:, c * 512:c * 512 + wnd], pb[sbank][:, 0:wnd], 1.0, None, ALU.mult, ALU.max,
                        accum_out=st[:, C_CMAX + c:C_CMAX + c + 1]),
                        r=[PS(sbank)], w=[("sc", kk) for kk in range(c * 4, c * 4 + wnd // 128)] + [("cmax", j)])
                if is_last:
                    P.op("dve", lambda e, sbank=sbank, wnd=wnd, i=i: e.tensor_tensor(
                        score[:, i * 128:(i + 1) * 128], pb[sbank][:, wnd:wnd + 128], causb[:], ALU.add),
                        r=[PS(sbank), "causb"], w=[("sc", i)])
                    P.op("dve", lambda e, i=i: e.tensor_reduce(st[:, C_CMAX + 8:C_CMAX + 9], score[:, i * 128:(i + 1) * 128],
                                                                AX.X, ALU.max), r=[("sc", i)], w=[("cmax", j)])
            sck = [("sc", kk) for kk in range(i + 1)]
            if i >= 2:
                P.op("dve", lambda e: e.tensor_reduce(st[:, C_RMAX:C_RMAX + 1], st[:, C_CMAX:C_CMAX + 9], AX.X, ALU.max),
                     r=[("cmax", j)], w=[("rmax", j)])
                P.op("dve", lambda e, i=i, Mb=Mb: e.tensor_scalar(Mb[:, 0:i * 128], score[:, 0:i * 128], 1.0, None, ALU.mult,
                                                           ALU.min, accum_out=st[:, C_LO:C_LO + 1]),
                     r=sck, w=[mbk, ("lo", j)])
                P.op("dve", lambda e: e.tensor_tensor(st[:, C_W0:C_W0 + 1], st[:, C_RMAX:C_RMAX + 1], st[:, C_LO:C_LO + 1],
                                                      ALU.subtract), r=[("rmax", j), ("lo", j)], w=[("w0", j)])
                nd = max(128, int(round((0.444 * n - 430.0) / 128.0)) * 128)
                nd = min(nd, n - 128)
                na = n - nd
                P.op("dve", lambda e, W0C=W0C: e.tensor_scalar(W0C, ctab[:], st[:, C_W0:C_W0 + 1], None, ALU.mult),
                     r=[("w0", j), "ctab"], w=[("w0c", j)])
                P.op("dve", lambda e: e.tensor_scalar(st[:, C_MID:C_MID + 1], st[:, C_W0:C_W0 + 1], 0.5,
                                                      st[:, C_LO:C_LO + 1], ALU.mult, ALU.add),
                     r=[("w0", j), ("lo", j)], w=[("mid", j)])
                for k in range(NB):
                    P.op("dve", lambda e, nd=nd, Mb=Mb: e.tensor_scalar(Mb[:, 0:nd], score[:, 0:nd], st[:, C_MID:C_MID + 1], None,
                                                                 ALU.is_ge, ALU.add, accum_out=st[:, C_CNT:C_CNT + 1]),
                         r=sck + [("mid", j)], w=[("MbD", m3), ("cnt", j)])
                    P.op("act", lambda e, nd=nd, n=n, Mb=Mb: e.activation(Mb[:, nd:n], score[:, nd:n], AF.Sign,
                                                                   bias=st[:, C_MID:C_MID + 1], scale=-1.0,
                                                                   accum_out=st[:, C_SA:C_SA + 1]),
                         r=sck + [("mid", j)], w=[("MbA", m3), ("sa", j)])
                    P.op("dve", lambda e: e.scalar_tensor_tensor(st[:, C_TMP:C_TMP + 1], st[:, C_CNT:C_CNT + 1], 2.0,
                                                                 st[:, C_SA:C_SA + 1], ALU.mult, ALU.subtract),
                         r=[("cnt", j), ("sa", j)], w=[("tmp", j)])
                    P.op("dve", lambda e, na=na, k=k, W0C=W0C: e.tensor_scalar(st[:, C_T2:C_T2 + 1], st[:, C_TMP:C_TMP + 1],
                                                                 float(2 * TOPK - na), W0C[:, k:k + 1], ALU.is_ge, ALU.mult),
                         r=[("tmp", j), ("w0c", j)], w=[("t2", j)])
                    kk2 = k + 1 if k + 1 < NB else k
                    dstc = C_MID if k + 1 < NB else C_LO
                    P.op("dve", lambda e, kk2=kk2, dstc=dstc, W0C=W0C: e.tensor_scalar(
                        st[:, dstc:dstc + 1], st[:, C_MID:C_MID + 1], W0C[:, kk2:kk2 + 1], st[:, C_T2:C_T2 + 1],
                        ALU.subtract, ALU.add),
                        r=[("mid", j), ("w0c", j), ("t2", j)], w=[("mid", j) if k + 1 < NB else ("lo", j)])
                lo_ap = st[:, C_LO:C_LO + 1]
                lok = ("lo", j)
            else:
                lo_ap = st[:, C_CONST:C_CONST + 1]
                lok = "st_const"
            P.op("dve", lambda e, n=n, lo_ap=lo_ap: e.tensor_scalar(Mb[:, 0:n], score[:, 0:n], lo_ap, None, ALU.is_ge),
                 r=sck + [lok], w=[mbk])
            if stop == "index":
                if g == DG_:
                    dump(f"score{j}", score[:, 0:n], sck, [128, n])
                    dump(f"Mb{j}", Mb[:, 0:n], [mbk], [128, n], BF16)
                    dump(f"st{j}", st[:], [("lo", j), ("cnt", j), ("rmax", j), ("cmax", j)] if i >= 2 else [("cmax", j)], [128, 64])
                return
        if part == "attn":
            for jt in range(i + 1):
                ms = jt % 2
                transposes([Mb[:, jt * 128:(jt + 1) * 128]], [mbk],
                           lambda e, ps, ms=ms: act_copy(e, maskT[ms][:], ps), [("maskT", ms)])
                sb0 = 1 if (jt % 2 == 0) else 3
                for h in range(8):
                    ee = h % 2
                    pp = h // 2
                    bank = sb0 + ee
                    P.op("pe", lambda e, bank=bank, ee=ee, pp=pp, jt=jt, j=j: e.matmul(
                        pb[bank][:, pp * 128:(pp + 1) * 128], KT[ee * 64:(ee + 1) * 64, pp, jt * 128:(jt + 1) * 128],
                        QT[ee * 64:(ee + 1) * 64, j, pp, :], start=True, stop=True),
                        r=[("KT", jt), ("QT", j)], w=[PS(bank)])
                eb = jt % 2
                for hh in range(2):
                    P.op("act", lambda e, eb=eb, hh=hh, sb0=sb0: e.activation(E[eb][:, hh * 512:(hh + 1) * 512], pb[sb0 + hh][:],
                                                                             AF.Exp, scale=0.125),
                         r=[PS(sb0 + hh)], w=[("E", eb, hh)])
                P.op("dve", lambda e, eb=eb, ms=ms: e.tensor_tensor(
                    E[eb][:].rearrange("p (h t) -> p h t", h=8), E[eb][:].rearrange("p (h t) -> p h t", h=8),
                    maskT[ms][:].unsqueeze(1).to_broadcast([128, 8, 128]), ALU.mult),
                    r=[("E", eb, 0), ("E", eb, 1), ("maskT", ms)], w=[("E", eb, 0), ("E", eb, 1)])
                for blk in range(8):
                    ee = blk // 4
                    pp = blk % 4
                    h = 2 * pp + ee
                    bank = 5 + ee
                    c0 = pp * 65
                    P.op("pe", lambda e, bank=bank, c0=c0, h=h, blk=blk, eb=eb, jt=jt, pp=pp, i=i: e.matmul(
                        pb[bank][:, c0:c0 + 65], E[eb][:, blk * 128:(blk + 1) * 128], Vc[:, jt, h, :],
                        start=(jt == 0 and pp == 0), stop=(jt == i), skip_group_check=True),
                        r=[("E", eb, 0), ("E", eb, 1), ("Vc", jt)], w=[PS(bank)])
            for ee in range(2):
                ov = pb[5 + ee][:, 0:260].rearrange("p (h c) -> p h c", h=4)
                ybv = yb[:].rearrange("p (pp ee f) -> p ee pp f", pp=4, ee=2)[:, ee]
                P.op("dve", lambda e, ov=ov, ee=ee: e.tensor_copy(st[:, C_RDEN + ee * 4:C_RDEN + ee * 4 + 4], ov[:, :, 64]),
                     r=[PS(5 + ee)], w=["rden"])
                P.op("dve", lambda e, ee=ee: e.reciprocal(st[:, C_RDEN + ee * 4:C_RDEN + ee * 4 + 4],
                                                          st[:, C_RDEN + ee * 4:C_RDEN + ee * 4 + 4]), r=["rden"], w=["rden"])
                P.op("dve", lambda e, ov=ov, ee=ee, ybv=ybv: e.tensor_tensor(
                    ybv, ov[:, :, 0:64],
                    st[:, C_RDEN + ee * 4:C_RDEN + ee * 4 + 4].unsqueeze(2).to_broadcast([128, 4, 64]), ALU.mult),
                    r=[PS(5 + ee), "rden"], w=["TB1"])
            transposes([yb[:, q * 128:(q + 1) * 128] for q in range(4)], ["TB1"],
                       lambda e, ps: act_copy(e, ybT[:].rearrange("p q t -> p (q t)"), ps), ["ybT"])
            if stop == "attn":
                if g == DG_:
                    dump(f"yb{j}", yb[:], ["TB1"], [128, 512], BF16)
                return
        if part == "merge":
            for hh in range(2):
                for kc in range(4):
                    P.op("pe", lambda e, hh=hh, kc=kc, WA=WA: e.matmul(pb[1 + hh][:], yaT[:, kc, :], WA[:, kc, hh * 512:(hh + 1) * 512],
                                                                start=(kc == 0), stop=(kc == 3)),
                         r=["yaT", wak], w=[PS(1 + hh)])
                for kc in range(4):
                    P.op("pe", lambda e, hh=hh, kc=kc, WB=WB: e.matmul(pb[3 + hh][:], ybT[:, kc, :], WB[:, kc, hh * 512:(hh + 1) * 512],
                                                                start=(kc == 0), stop=(kc == 3)),
                         r=["ybT", wbk], w=[PS(3 + hh)])
                P.op("dve", lambda e, hh=hh, j=j: e.scalar_tensor_tensor(T[2][:], th[:, j, hh * 512:(hh + 1) * 512], 1.0,
                                                                         pb[1 + hh][:], ALU.add, ALU.mult),
                     r=[("th", j, 7 + hh), PS(1 + hh)], w=["T2"])
                P.op("dve", lambda e, hh=hh, j=j: e.scalar_tensor_tensor(T[1][:], th[:, j, 1024 + hh * 512:1024 + (hh + 1) * 512],
                                                                         1.0, pb[3 + hh][:], ALU.add, ALU.mult),
                     r=[("th", j, 9 + hh), PS(3 + hh)], w=["T1"])
                P.op("dve", lambda e, hh=hh: e.scalar_tensor_tensor(mbf[:, hh * 512:(hh + 1) * 512], T[2][:], 0.5, T[1][:],
                                                                    ALU.mult, ALU.add), r=["T2", "T1"], w=["hbf"])
            transposes([mbf[:, kc * 128:(kc + 1) * 128] for kc in range(8)], ["hbf"],
                       lambda e, ps: act_copy(e, mT[:].rearrange("p k t -> p (k t)"), ps), ["mT"])
            for hh in range(2):
                WO, wok = load_w(wt_o[hh].rearrange("p (kc n) -> p kc n", kc=8), (128, 8, 512),
                                 ("wb_o", hh), slot=sC)
                for kc in range(8):
                    P.op("pe", lambda e, hh=hh, kc=kc, WO=WO: e.matmul(pb[5 + hh][:], mT[:, kc, :], WO[:, kc, :],
                                                                       start=(kc == 0), stop=(kc == 7)),
                         r=["mT", wok], w=[PS(5 + hh)])
                P.op("dve", lambda e, hh=hh, j=j: e.scalar_tensor_tensor(xs[:, j, hh * 512:(hh + 1) * 512], pb[5 + hh][:], 0.5,
                                                                         xs[:, j, hh * 512:(hh + 1) * 512], ALU.mult, ALU.add),
                     r=[PS(5 + hh), ("xs", j)], w=[("xs", j)])
            if stop == "merge":
                if g == DG_:
                    dump(f"mbf{j}", mbf[:], ["hbf"], [128, D], BF16)
                    dump(f"thd{j}", th[:, j, :], [("th", j, b) for b in (7, 8, 9, 10)], [128, 2048], BF16)
                    dump(f"yaT{j}", yaT[:], ["yaT"], [128, 4, 128], BF16)
                    dump(f"ybT{j}", ybT[:], ["ybT"], [128, 4, 128], BF16)
                    dump(f"x1_{j}", xs[:, j, :], [("xs", j)], [128, D])
                return
            rmsnorm_to_T(j, g2buf, "g2buf", xs[:, j, :], [("xs", j)], hT2[g % 2], slice(j * 128, (j + 1) * 128), ("hT", g % 2, j))

    def ffn_and_store(gp):
        hTb = hT2[gp % 2]
        hk = [("hT", gp % 2, j) for j in range(G)]
        for fb in range(8):
            W1, w1k = load_w(wt_f1[fb].rearrange("p (kc n) -> p kc n", kc=8), (128, 8, 512),
                             ("wb_f1", fb))
            W2, w2k = load_w(wt_f2[fb].rearrange("p (fc n) -> p fc n", fc=4),
                             (128, 4, 1024), ("wb_f2", fb))
            for fcl in range(4):
                fc = fb * 4 + fcl
                hbank = 1 + (fc % 2)
                hb = fc % 2
                for kc in range(8):
                    P.op("pe", lambda e, hbank=hbank, kc=kc, fcl=fcl, W1=W1, hTb=hTb: e.matmul(
                        pb[hbank][:, 0:GT], W1[:, kc, fcl * 128:(fcl + 1) * 128], hTb[:, kc, :],
                        start=(kc == 0), stop=(kc == 7)), r=hk + [w1k], w=[PS(hbank)])
                tkey = "T2a" if hb == 0 else "T2b"
                P.op("act", lambda e, hbank=hbank, hb=hb: e.activation(T2_[:, hb * GT:(hb + 1) * GT], pb[hbank][:, 0:GT], AF.Relu),
                     r=[PS(hbank)], w=[tkey])
                hd = fc % 4
                P.op("pool", lambda e, hb=hb, hd=hd: e.tensor_tensor(HD[:, hd, :], T2_[:, hb * GT:(hb + 1) * GT],
                                                                     T2_[:, hb * GT:(hb + 1) * GT], ALU.mult),
                     r=[tkey], w=[("HD", hd)])
                for j in range(G):
                    for half in range(2):
                        obank = 3 + 2 * j + half
                        P.op("pe", lambda e, obank=obank, fc=fc, fcl=fcl, j=j, W2=W2, hd=hd, half=half: e.matmul(
                            pb[obank][:], HD[:, hd, j * 128:(j + 1) * 128], W2[:, fcl, half * 512:(half + 1) * 512],
                            start=(fc == 0), stop=(fc == 31)), r=[("HD", hd), w2k], w=[PS(obank)])
        for j in range(G):
            for half in range(2):
                obank = 3 + 2 * j + half
                P.op("dve", lambda e, obank=obank, j=j, half=half: e.tensor_tensor(
                    xs[:, j, half * 512:(half + 1) * 512], xs[:, j, half * 512:(half + 1) * 512], pb[obank][:], ALU.add),
                    r=[PS(obank), ("xs", j)], w=[("xs", j)])
        P.dma("pool", [lambda e: e.dma_start(out=T01[:], in_=gf.partition_broadcast(128))], "xtmp", [], ["T0", "T1"])
        for j in range(G):
            i = gp * G + j
            xk = ("xs", j)
            P.op("act", lambda e, j=j: e.activation(hbf[:], xs[:, j, :], AF.Square, accum_out=st[:, C_SS + j:C_SS + j + 1]),
                 r=[xk], w=["hbf", ("ss", j)])
            rsqrt_ops(C_SS + j, C_RS + j, 1.0 / D, EPS, [("ss", j)], [("rs", j)])
            P.op("dve", lambda e, j=j: e.scalar_tensor_tensor(xs[:, j, :], xs[:, j, :], st[:, C_RS + j:C_RS + j + 1], T01[:],
                                                              ALU.mult, ALU.mult), r=[xk, ("rs", j), "T0", "T1"], w=[xk])
            tok = P.dma("pool", [lambda e, i=i, j=j: e.dma_start(out=y[i * 128:(i + 1) * 128, :], in_=xs[:, j, :])],
                        f"out{j}", [xk], [])
            final_waits.append(tok)

    def early_part(g):
        for j in range(G):
            i = g * G + j
            P.dma("pool", [lambda e, i=i: e.dma_start(out=T01[:], in_=x[i * 128:(i + 1) * 128, :])],
                  "xtmp", [], ["T0", "T1"])
            P.dma("pool", [lambda e, i=i, j=j: e.dma_start(out=cs[:, j, :], in_=c_rope[i * 128:(i + 1) * 128, :])],
                  f"cs{j}", [], [("cs", j)])
            rmsnorm_to_T(j, g1buf, "g1buf", T01[:], ["T0", "T1"], hT2[g % 2], slice(j * 128, (j + 1) * 128), ("hT", g % 2, j))
        if g == 0:
            for (dst_, src_, key_) in CONV_LATER:
                conv(dst_, src_, key_)
        for blk_ in PROJ_BLOCKS[:2]:
            proj_block(g, *blk_)

    for g in range(NG):
        if g == 0:
            early_part(0)
            tile_part(0, 0, "index")
            tile_part(0, 1, "index")
        for j in range(G):
            i = g * G + j
            P.dma("pool", [lambda e, i=i, j=j: e.dma_start(out=xs[:, j, :], in_=x[i * 128:(i + 1) * 128, :])],
                  f"xs{j}", [], [("xs", j)])
        for blk_ in PROJ_BLOCKS[2:]:
            proj_block(g, *blk_)
        sA = slot_ctr[0] % NSLOT
        sB = (slot_ctr[0] + 1) % NSLOT
        sC = (slot_ctr[0] + 2) % NSLOT
        slot_ctr[0] += 3
        reserved.update((sA, sB, sC))
        stream_slot[0] = sC
        WA, wak = load_w(wt_oa.rearrange("p (kc n) -> p kc n", kc=4), (128, 4, 1024), ("wb_oa",), slot=sA)
        WB, wbk = load_w(wt_ob.rearrange("p (kc n) -> p kc n", kc=4), (128, 4, 1024), ("wb_ob",), slot=sB)
        tile_part(g, 0, "sgu")
        if g + 1 < NG:
            early_part(g + 1)
        tile_part(g, 0, "attn")
        tile_part(g, 0, "merge")
        tile_part(g, 1, "sgu")
        tile_part(g, 1, "attn")
        tile_part(g, 1, "merge")
        reserved.clear()
        if g + 1 < NG:
            tile_part(g + 1, 0, "index")
        ffn_and_store(g)
        if g + 1 < NG:
            tile_part(g + 1, 1, "index")

    final_waits.extend(dbg_ids)
    P.emit(final_waits)
    return nc, es, list(dbg_out.keys())


def host_consts(S):
    ident = np.eye(128, dtype=np.float32).astype(ml_dtypes.bfloat16)
    t = np.arange(128)[:, None]
    s = np.arange(128)[None, :]
    caus = np.where(s <= t, 0.0, NEG).astype(np.float32)
    tril = (s <= t).astype(np.float32)
    pos = np.arange(S, dtype=np.float32)
    inv_freq = (np.float32(10000.0) ** (-np.arange(0, 64, 2, dtype=np.float32) / np.float32(64))).astype(np.float32)
    ang = (pos[:, None] * inv_freq[None, :]).astype(np.float32)
    cos = np.cos(ang).astype(np.float32)
    sin = np.sin(ang).astype(np.float32)
    rope = np.concatenate([cos, sin, -sin], axis=1).astype(np.float32)
    return ident, caus, tril, rope


def make_in_maps(inputs, NT, ncores=8):
    S = NT * 128
    ident, caus, tril, rope = host_consts(S)
    f = lambda a: np.ascontiguousarray(np.asarray(a, dtype=np.float32))
    shared = {
        "g1": f(inputs["norm1_g"]).reshape(1, D),
        "w_in": f(inputs["w_in"])[0],
        "w_s": f(inputs["w_s"])[0],
        "b_s": f(inputs["b_s"])[0],
        "ln_g": f(inputs["sgu_ln_g"]).reshape(1, 512),
        "ln_b": f(inputs["sgu_ln_b"]).reshape(1, 512),
        "w_oa": f(inputs["w_out_a"])[0],
        "w_ob": f(inputs["w_out_b"])[0],
        "w_o": f(inputs["w_o"])[0],
        "g2": f(inputs["norm2_g"]).reshape(1, D),
        "w_f1": f(inputs["w_ff_in"])[0],
        "w_f2": f(inputs["w_ff_out"])[0],
        "gf": f(inputs["norm_f_g"]).reshape(1, D),
        "c_ident": ident, "c_caus": caus, "c_tril": tril, "c_rope": rope,
    }
    xx = np.asarray(inputs["x"], dtype=np.float32)
    maps = []
    for c in range(ncores):
        m = dict(shared)
        m["x"] = np.ascontiguousarray(xx[c, :S, :])
        maps.append(m)
    return maps


_CACHE = {}


def kernel(**inputs):
    NT = 32
    if "nc" not in _CACHE:
        _CACHE["nc"] = build(NT)
    nc, es, _ = _CACHE["nc"]
    in_maps = make_in_maps(inputs, NT)
    res = run_bass_kernel_spmd(nc, in_maps, core_ids=list(range(8)))
    out = np.stack([np.asarray(r["y"], dtype=np.float32) for r in res.results], axis=0)
    return out
```
